# Optimizing a Trainium2 kernel written in Bass

```python
import math
import jax, jax.numpy as jnp
from jax import lax
import numpy as np

D_MODEL = 1024
BATCH = 8
SEQ = 4096
DEPTH = 1

HG_HEADS = 8
HG_DK = 128
HG_DV = 128
HG_WIDTH = HG_HEADS * HG_DK
HG_CHUNK = 32
POOL_WINDOWS = (2, 4, 8, 16)
POOL_GROUPS = 4
POOL_GROUP = 128
POOL_WIDTH = POOL_GROUPS * POOL_GROUP
POOL_OUT_GROUP = D_MODEL // POOL_GROUPS
MAX_WIN = 16
IN_WIDTHS = (HG_WIDTH, HG_WIDTH, HG_WIDTH, HG_WIDTH, POOL_WIDTH, D_MODEL, D_MODEL)
IN_COLS = sum(IN_WIDTHS)
IN_SPLITS = tuple(int(s) for s in np.cumsum(IN_WIDTHS)[:-1])
PEER_HEADS = 8
PEER_NKEYS = 128
PEER_EXPERTS = PEER_NKEYS * PEER_NKEYS
PEER_QDIM = 256
PEER_HALF = PEER_QDIM // 2
PEER_TOPK = 16
PEER_TOKEN_BLOCK = 128
PLE_DIM = 256
ALPHA = (2.0 * DEPTH) ** 0.25
BETA = (8.0 * DEPTH) ** -0.25
LN_EPS = 1e-5
RMS_EPS = 1e-6

kernel_name = "hybrid_hgrn2_pool_peer_deepnorm_block"


def layer_norm(x, g, b):
    xf = x.astype(jnp.float32)
    mu = jnp.mean(xf, axis=-1, keepdims=True)
    var = jnp.mean(jnp.square(xf - mu), axis=-1, keepdims=True)
    return ((xf - mu) * lax.rsqrt(var + LN_EPS) * g + b).astype(x.dtype)


def hgrn2_mixer(q, f_logit, i, g, lb, norm_g):
    B, S, _ = q.shape
    f32 = jnp.float32
    nc = S // HG_CHUNK
    f = lb + (1.0 - lb) * jax.nn.sigmoid(f_logit.astype(f32))
    k = 1.0 - f
    logf = jnp.log(f)

    def heads(t, d):
        return t.reshape(B, nc, HG_CHUNK, HG_HEADS, d).transpose(0, 3, 1, 2, 4)

    qh = heads(jax.nn.silu(q.astype(f32)) * (HG_DK ** -0.5), HG_DK)
    kh = heads(k, HG_DK)
    vh = heads(i.astype(f32), HG_DV)
    bh = jnp.cumsum(heads(logf, HG_DK), axis=3)
    q_dec = qh * jnp.exp(bh)
    k_inv = kh * jnp.exp(-bh)
    causal = jnp.tril(jnp.ones((HG_CHUNK, HG_CHUNK), dtype=bool))
    scores = jnp.where(causal, jnp.einsum('bhncd,bhnsd->bhncs', q_dec, k_inv), 0.0)
    o_intra = jnp.einsum('bhncs,bhnse->bhnce', scores, vh)
    b_last = bh[:, :, :, -1:, :]
    k_end = kh * jnp.exp(b_last - bh)
    d_state = jnp.einsum('bhncd,bhnce->nbhde', k_end, vh)
    chunk_decay = jnp.exp(b_last[:, :, :, 0, :]).transpose(2, 0, 1, 3)

    def step(state, inp):
        dec, ds = inp
        return dec[..., None] * state + ds, state

    s0 = jnp.zeros((B, HG_HEADS, HG_DK, HG_DV), f32)
    _, s_in = lax.scan(step, s0, (chunk_decay, d_state))
    o_inter = jnp.einsum('bhncd,nbhde->bhnce', q_dec, s_in)
    o = (o_intra + o_inter).transpose(0, 2, 3, 1, 4).reshape(B, S, HG_HEADS, HG_DV)
    o = o * lax.rsqrt(jnp.mean(jnp.square(o), axis=-1, keepdims=True) + RMS_EPS) * norm_g
    o = o.reshape(B, S, HG_HEADS * HG_DV) * jax.nn.silu(g.astype(f32))
    return o.astype(q.dtype)


def pool_mixer(v, w_grp, scale):
    B, S, _ = v.shape
    vf = v.astype(jnp.float32)
    c = jnp.pad(jnp.cumsum(vf, axis=1), ((0, 0), (MAX_WIN, 0), (0, 0)))
    pos = jnp.arange(S)
    outs = []
    for gi, w in enumerate(POOL_WINDOWS):
        lo, hi = gi * POOL_GROUP, (gi + 1) * POOL_GROUP
        wsum = c[:, MAX_WIN:, lo:hi] - c[:, MAX_WIN - w:MAX_WIN - w + S, lo:hi]
        cnt = jnp.minimum(pos + 1, w).astype(jnp.float32)[None, :, None]
        outs.append(wsum / cnt - vf[:, :, lo:hi])
    pooled = jnp.stack(outs, axis=2).astype(v.dtype)
    y = jnp.einsum('bsgc,gco->bsgo', pooled, w_grp).reshape(B, S, D_MODEL)
    return y * scale


def peer_ffn(x, w_query, sub_keys, u_tab, v_tab):
    B, S, D = x.shape
    q = (x @ w_query).reshape(B, S, PEER_HEADS, 2, PEER_HALF)
    sc = jnp.einsum('bshpk,hpnk->bshpn', q, sub_keys).astype(jnp.float32)
    top_v, top_i = lax.top_k(sc, PEER_TOPK)
    cand = top_v[..., 0, :, None] + top_v[..., 1, None, :]
    cand = cand.reshape(B, S, PEER_HEADS, PEER_TOPK * PEER_TOPK)
    best_v, best_pos = lax.top_k(cand, PEER_TOPK)
    i1 = jnp.take_along_axis(top_i[..., 0, :], best_pos // PEER_TOPK, axis=-1)
    i2 = jnp.take_along_axis(top_i[..., 1, :], best_pos % PEER_TOPK, axis=-1)
    expert = i1 * PEER_NKEYS + i2
    gate = jax.nn.softmax(best_v, axis=-1).astype(x.dtype)
    nb = (B * S) // PEER_TOKEN_BLOCK
    hk = PEER_HEADS * PEER_TOPK
    xb = x.reshape(nb, PEER_TOKEN_BLOCK, D)
    eb = expert.reshape(nb, PEER_TOKEN_BLOCK, hk)
    gb = gate.reshape(nb, PEER_TOKEN_BLOCK, hk)

    def block(args):
        xt, et, gt = args
        u = u_tab[et]
        act = jax.nn.gelu(jnp.einsum('td,tkd->tk', xt, u), approximate=False) * gt
        return jnp.einsum('tk,tkd->td', act, v_tab[et])

    y = lax.map(block, (xb, eb, gb))
    return y.reshape(B, S, D)


def setup_inputs(seed: int = 0) -> dict:
    key = jax.random.key(seed)
    ks = jax.random.split(key, 24)
    nrm = jax.random.normal
    f32 = jnp.float32
    L = DEPTH
    return {
        "x": nrm(ks[0], (BATCH, SEQ, D_MODEL), f32),
        "p": nrm(ks[1], (DEPTH, BATCH, SEQ, PLE_DIM), f32),
        "ln0_g": 1.0 + 0.02 * nrm(ks[2], (D_MODEL,), f32),
        "ln0_b": 0.02 * nrm(ks[3], (D_MODEL,), f32),
        "w_in": nrm(ks[4], (L, D_MODEL, IN_COLS), f32) * D_MODEL ** -0.5,
        "hg_lb": 1.0 + 0.1 * nrm(ks[5], (DEPTH + 1, HG_WIDTH), f32),
        "hg_norm_g": 1.0 + 0.02 * nrm(ks[6], (L, HG_DV), f32),
        "w_hg_branch": nrm(ks[7], (L, HG_WIDTH, D_MODEL), f32) * (BETA * HG_WIDTH ** -0.5),
        "pool_w": nrm(ks[8], (L, POOL_GROUPS, POOL_GROUP, POOL_OUT_GROUP), f32) * (BETA * POOL_GROUP ** -0.5),
        "pool_scale": 1.0 + 0.1 * nrm(ks[9], (L, D_MODEL), f32),
        "w_out": nrm(ks[10], (L, D_MODEL, D_MODEL), f32) * (BETA * D_MODEL ** -0.5),
        "ln1_g": 1.0 + 0.02 * nrm(ks[11], (L, D_MODEL), f32),
        "ln1_b": 0.02 * nrm(ks[12], (L, D_MODEL), f32),
        "w_query": nrm(ks[13], (L, D_MODEL, PEER_HEADS * PEER_QDIM), f32) * D_MODEL ** -0.5,
        "sub_keys": nrm(ks[14], (L, PEER_HEADS, 2, PEER_NKEYS, PEER_HALF), f32) * PEER_HALF ** -0.5,
        "u_tab": nrm(ks[15], (L, PEER_EXPERTS, D_MODEL), f32) * D_MODEL ** -0.5,
        "v_tab": nrm(ks[16], (L, PEER_EXPERTS, D_MODEL), f32) * (BETA * PEER_HEADS ** -0.5),
        "w_ple_gate": nrm(ks[17], (L, D_MODEL, D_MODEL), f32) * D_MODEL ** -0.5,
        "w_ple_proj": nrm(ks[18], (L, PLE_DIM, D_MODEL), f32) * (BETA * PLE_DIM ** -0.5),
        "ln2_g": 1.0 + 0.02 * nrm(ks[19], (L, D_MODEL), f32),
        "ln2_b": 0.02 * nrm(ks[20], (L, D_MODEL), f32),
    }


def reference(x, p, ln0_g, ln0_b, w_in, hg_lb, hg_norm_g, w_hg_branch, pool_w, pool_scale,
              w_out, ln1_g, ln1_b, w_query, sub_keys, u_tab, v_tab, w_ple_gate, w_ple_proj,
              ln2_g, ln2_b):
    h = layer_norm(x, ln0_g, ln0_b)
    lb_all = jnp.cumsum(jax.nn.softmax(hg_lb.astype(jnp.float32), axis=0), axis=0)
    for l in range(DEPTH):
        proj = h @ w_in[l]
        q, f_logit, i_val, g_out, v_pool, gate_a, gate_b = jnp.split(proj, IN_SPLITS, axis=-1)
        y_a = hgrn2_mixer(q, f_logit, i_val, g_out, lb_all[l], hg_norm_g[l]) @ w_hg_branch[l]
        y_b = pool_mixer(v_pool, pool_w[l], pool_scale[l])
        mix = jax.nn.sigmoid(gate_a) * y_a + jax.nn.sigmoid(gate_b) * y_b
        h = layer_norm(ALPHA * h + mix @ w_out[l], ln1_g[l], ln1_b[l])
        ple = jax.nn.sigmoid(h @ w_ple_gate[l]) * (p[l] @ w_ple_proj[l])
        ffn = peer_ffn(h, w_query[l], sub_keys[l], u_tab[l], v_tab[l])
        h = layer_norm(ALPHA * h + ffn + ple, ln2_g[l], ln2_b[l])
    return h
```

```python
from contextlib import ExitStack
import numpy as np
import concourse.bass as bass
import concourse.mybir as mybir
from concourse.bass_utils import run_bass_kernel_spmd

F32 = mybir.dt.float32
BF16 = mybir.dt.bfloat16
U32 = mybir.dt.uint32
I32 = mybir.dt.int32
ALU = mybir.AluOpType
AF = mybir.ActivationFunctionType
AX = mybir.AxisListType

SEQ = 4096
D = 1024
ALPHA = 2.0 ** 0.25
NCHUNK_W = 23
SB_BASE = 16512
SB_END = 229376
HG_CH = 64
STRICT_SAME_ENGINE = True


class Sched:
    def __init__(self):
        self.ops = []
        self.last_w = {}
        self.readers = {}

    def add(self, eng, fn, r=(), w=(), dsem=None):
        idx = len(self.ops)
        deps = set()
        for res in r:
            lw = self.last_w.get(res)
            if lw is not None:
                deps.add((lw, 'raw'))
        for res in w:
            lw = self.last_w.get(res)
            if lw is not None:
                deps.add((lw, 'waw'))
            for rd in self.readers.get(res, ()):
                deps.add((rd, 'war'))
        fdeps = set()
        for d, kind in deps:
            de = self.ops[d]['eng']
            if de == eng and eng != 'sp' and self.ops[d]['dsem'] is None and dsem is None:
                if eng == 'pe' or (kind != 'raw' and not STRICT_SAME_ENGINE):
                    continue
            fdeps.add(d)
        self.ops.append(dict(eng=eng, fn=fn, deps=fdeps, dsem=dsem))
        for res in r:
            self.readers.setdefault(res, []).append(idx)
        for res in w:
            self.last_w[res] = idx
            self.readers[res] = []
        return idx

    def emit(self, nc, stack):
        ops = self.ops
        needed = set()
        for op in ops:
            needed |= op['deps']
        sems = {}

        def getsem(key):
            if key not in sems:
                sems[key] = stack.enter_context(nc.semaphore("s%d" % len(sems)))
            return sems[key]

        cnt = {}
        tot = {}
        for op in ops:
            if op['dsem'] is not None:
                tot[op['dsem'][0]] = tot.get(op['dsem'][0], 0) + 1
        for i, op in enumerate(ops):
            if op['dsem'] is not None:
                key, mode = op['dsem']
                cnt[key] = cnt.get(key, 0) + 1
                op['sem'] = getsem(('d', key))
                op['tick'] = 16 * (cnt[key] if mode == 'slot' else tot[key])
                op['inc'] = 16
            elif i in needed:
                e = op['eng']
                cnt[e] = cnt.get(e, 0) + 1
                op['sem'] = getsem(('e', e))
                op['tick'] = cnt[e]
                op['inc'] = 1
            else:
                op['inc'] = 0
        self.nsems = len(sems)
        per = {e: [] for e in ('pe', 'act', 'dve', 'pool', 'sp')}
        for op in ops:
            per[op['eng']].append(op)
        block = stack.enter_context(nc.Block())

        def run(eh, lst):
            waited = {}
            for op in lst:
                w = {}
                for d in op['deps']:
                    dop = ops[d]
                    s = dop['sem']
                    if dop['tick'] > w.get(s, 0):
                        w[s] = dop['tick']
                for s, t in w.items():
                    if t > waited.get(s, 0):
                        eh.wait_ge(s, t)
                        waited[s] = t
                ins = op['fn'](eh)
                if op['inc']:
                    ins.then_inc(op['sem'], op['inc'])

        @block.tensor
        def _(e):
            run(e, per['pe'])

        @block.scalar
        def _(e):
            run(e, per['act'])

        @block.vector
        def _(e):
            run(e, per['dve'])

        @block.gpsimd
        def _(e):
            run(e, per['pool'])

        @block.sync
        def _(e):
            run(e, per['sp'])


class Arena:
    def __init__(self, nc, base, end):
        self.nc, self.base, self.end, self.n = nc, base, end, 0

    def at(self, off, name, shape, dt):
        self.n += 1
        return self.nc.alloc_sbuf_tensor_at("%s_%d" % (name, self.n), shape, dt, offset=off)


def _nbytes(shape, dt):
    n = 1
    for s in shape[1:]:
        n *= s
    return n * (2 if dt == BF16 else 4)


class Bump:
    def __init__(self, arena, start, end):
        self.a, self.p, self.end = arena, start, end
        self.off = {}

    def __call__(self, name, shape, dt):
        nb = (_nbytes(shape, dt) + 63) // 64 * 64
        off = self.p
        self.off[name] = off
        self.p += nb
        assert self.p <= self.end, ("SBUF overflow", name, self.p, self.end)
        return self.a.at(off, name, shape, dt)


def build_program(NTOK=SEQ, debug=False):
    NT = NTOK // 256
    nc = bass.Bass("TRN2", target_bir_lowering=False)
    dr = lambda name, shape, dt, kind: nc.dram_tensor(name, shape, dt, kind=kind).ap()
    x_d = dr("x", [NTOK, D], F32, "ExternalInput")
    p_d = dr("p", [NTOK, 256], F32, "ExternalInput")
    wbig_d = dr("wbig", [NCHUNK_W, 128, 8, 512], F32, "ExternalInput")
    wpool_d = dr("wpool", [128, 4, 256], F32, "ExternalInput")
    wpp_d = dr("wpp", [128, 2, 1024], F32, "ExternalInput")
    sk_d = dr("sk", [128, 16, 128], F32, "ExternalInput")
    u_d = dr("u_tab", [16384, D], F32, "ExternalInput")
    v_d = dr("v_tab", [16384, D], F32, "ExternalInput")
    cvec_d = dr("cvec", [128, 80], F32, "ExternalInput")
    cmat_d = dr("cmat", [128, 2324], F32, "ExternalInput")
    bc2_d = dr("bc2", [128, 2048], F32, "ExternalInput")
    y_d = dr("y", [NTOK, D], F32, "ExternalOutput")
    kd = "ExternalOutput" if debug else "Internal"
    wscr_d = dr("wscr", [NCHUNK_W, 128, 8, 512], BF16, "Internal")
    uT_d = dr("uT_s", [128, 128, 8, 128], BF16, "Internal")
    vs_d = dr("v_s", [128, 128, 1024], BF16, "Internal")
    h1T_d = dr("h1T_s", [128, 8, NTOK], BF16, kd)
    z0_d = dr("z0_s", [NTOK, D], F32, kd)

    S = Sched()
    st = ExitStack()
    A = Arena(nc, SB_BASE, SB_END)
    ps = nc.alloc_psum_tensor("ps", [128, 4096], F32)
    psb = ps[:, :].bitcast(BF16)

    def bank(b):
        return ps[:, b * 512:(b + 1) * 512]

    def bankb(b):
        return psb[:, b * 1024:(b + 1) * 1024]

    bctr = [0]

    def nb(n=1):
        if n == 2 and bctr[0] % 2 == 1:
            bctr[0] += 1
        b = bctr[0] % 8
        bctr[0] += n
        return b

    def PB(b):
        return 'pb%d' % b

    add = S.add

    def dma(out, in_, r, w, key, mode='slot'):
        add('sp', lambda e: e.dma_start(out=out, in_=in_), r=r, w=w, dsem=(key, mode))

    def mm(out, lhsT, rhs, start, stop, r, w):
        add('pe', lambda e: e.matmul(out, lhsT=lhsT, rhs=rhs, start=start, stop=stop), r=r, w=w)

    def tr(out, in_, ident, r, w):
        add('pe', lambda e: e.transpose(out=out, in_=in_, identity=ident), r=r, w=w)

    def act(out, in_, func, r, w, scale=1.0, bias=0.0):
        add('act', lambda e: e.activation(out=out, in_=in_, func=func, scale=scale, bias=bias), r=r, w=w)

    def cp(eng, out, in_, r, w):
        if eng == 'act':
            act(out, in_, AF.Copy, r, w)
        else:
            add(eng, lambda e: e.tensor_copy(out=out, in_=in_), r=r, w=w)

    def tt(eng, out, a, b, op, r, w):
        add(eng, lambda e: e.tensor_tensor(out=out, in0=a, in1=b, op=op), r=r, w=w)

    def ts(eng, out, a, s1, s2, op0, op1, r, w):
        add(eng, lambda e: e.tensor_scalar(out=out, in0=a, scalar1=s1, scalar2=s2, op0=op0, op1=op1), r=r, w=w)

    def stt(out, a, s, b, op0, op1, r, w):
        add('dve', lambda e: e.scalar_tensor_tensor(out=out, in0=a, scalar=s, in1=b, op0=op0, op1=op1), r=r, w=w)

    pers = Bump(A, SB_BASE, SB_END)
    cvec = pers("cvec", [128, 80], F32)
    identf = pers("identf", [128, 128], F32)
    identb = pers("identb", [128, 128], BF16)
    iotab = pers("iotab", [128, 128], BF16)
    iota16 = pers("iota16", [128, 16], F32)
    cmask4 = pers("cmask4", [128, 4], F32)
    I1T = pers("I1T", [128, NTOK], BF16)
    I2T = pers("I2T", [128, NTOK], BF16)
    GT = pers("GT", [128, NTOK], BF16)
    P_END = pers.p
    C_G0, C_B0, C_G1, C_B1, C_PSC, C_LB0, C_LB1, C_NG, C_OML, C_NOML = 0, 8, 16, 24, 32, 40, 48, 56, 57, 65
    M_ID, M_MASK, M_RESET, M_BANDS, M_I16, M_I128, M_CM4, M_ONES = 0, 128, 256, 512, 2048, 2064, 2192, 2196

    p0 = Bump(A, P_END, SB_END)
    cmat = p0("cmat", [128, 2324], F32)
    dma(cvec[:, :], cvec_d, [], ['cvec'], 'cvec')
    dma(cmat[:, :], cmat_d, [], ['cmat0'], 'cmat0')
    cp('dve', identf[:, :], cmat[:, M_ID:M_ID + 128], ['cmat0'], ['identf'])
    cp('dve', identb[:, :], cmat[:, M_ID:M_ID + 128], ['cmat0'], ['identb'])
    cp('dve', iotab[:, :], cmat[:, M_I128:M_I128 + 128], ['cmat0'], ['iotab'])
    cp('dve', iota16[:, :], cmat[:, M_I16:M_I16 + 16], ['cmat0'], ['iota16'])
    cp('dve', cmask4[:, :], cmat[:, M_CM4:M_CM4 + 4], ['cmat0'], ['cmask4'])
    tt('dve', cvec[:, C_OML:C_OML + 8], cvec[:, C_LB1:C_LB1 + 8], cvec[:, C_LB0:C_LB0 + 8], ALU.subtract, ['cvec'], ['cvec'])
    act(cvec[:, C_OML:C_OML + 8], cvec[:, C_OML:C_OML + 8], AF.Sigmoid, ['cvec'], ['cvec'])
    ts('dve', cvec[:, C_NOML:C_NOML + 8], cvec[:, C_OML:C_OML + 8], -1.0, None, ALU.mult, ALU.bypass, ['cvec'], ['cvec'])

    for c in (0, 1, 6, 7, 2, 3, 9, 10, 11, 12, 4, 5, 8, 13, 14, 15, 16, 21, 22, 17, 18, 19, 20):
        add('pool', lambda e, c=c: e.dma_start(out=wscr_d[c], in_=wbig_d[c]), w=['wscr%d' % c], dsem=('wc%d' % c, 'slot'))

    pa = Bump(A, P_END, SB_END)
    P0RES = ['cmat0']
    maskTb = pa("maskTb", [128, 128], BF16)
    resetm = pa("resetm", [128, 256], F32)
    bands = pa("bands", [128, 12, 128], BF16)
    onesb = pa("onesb", [128, 128], BF16)
    pwb = pa("pwb", [128, 4, 256], BF16)
    wppb = pa("wppb", [128, 2, 1024], BF16)
    skT = pa("skT", [128, 16, 128], BF16)
    wslot = [pa("wslot%d" % i, [128, 8, 512], BF16) for i in range(2)]
    xt = [pa("xt0", [128, 1024], F32)]
    eqb = pa("eqb", [128, 8, 16, 16], BF16)
    lnst = pa("lnst", [128, 2, 6], F32)
    lnmv = pa("lnmv", [128, 2], F32)
    lnr = pa("lnr", [128, 1], F32)
    hT32 = pa("hT32", [128, 8, 256], F32)
    hTb = pa("hTb", [128, 8, 256], BF16)
    qsT = pa("qsT", [128, 8, 256], BF16)
    sc = pa("sc", [128, 16, 128], F32)
    vtm = pa("vtm", [128, 2, 1024], BF16)
    sgT = pa("sgT", [128, 8, 256], BF16)
    vp = pa("vp", [128, 3, 512], BF16)
    gaT = pa("gaT", [128, 8, 256], BF16)
    gbT = pa("gbT", [128, 8, 256], BF16)
    E1 = pa("E1", [128, 8, 256], F32)
    E2 = pa("E2", [128, 8, 256], F32)
    bhT_off = pa.p
    bhT = pa("bhT", [128, 8, 256], F32)
    qpT = A.at(bhT_off, "qpT", [128, 16, 256], BF16)
    qdT = pa("qdT", [128, 8, 256], BF16)
    kiT = pa("kiT", [128, 8, 256], BF16)
    kendT = pa("kendT", [128, 8, 256], BF16)
    kendm = pa("kendm", [128, 4, 1024], BF16)
    scm = pa("scm", [128, 8, 128], BF16)
    S32 = pa("S32", [128, 8, 128], F32)
    Sbs = pa("Sbs", [128, 5, 8, 128], BF16)
    dec = pa("dec", [128, 8, 256 // HG_CH], F32)
    sqb = pa("sqb", [128, 8, 128], BF16)
    rstd = pa("rstd", [128, 8, 128], F32)
    on = pa("on", [128, 8, 128], F32)
    onT = pa("onT", [128, 8, 256], BF16)
    plT = pa("plT", [128, 4, 256], BF16)
    mixT = A.at(pa.off["kiT"], "mixT", [128, 8, 256], BF16)
    zb = A.at(pa.off["kendT"], "zb", [128, 8, 256], BF16)
    lnrs = pa("lnrs", [128, 256], F32)
    pld = pa("pld", [128, 2, 256], F32)
    pldb = pa("pldb", [128, 2, 256], BF16)
    pT = pa("pT", [128, 2, 256], BF16)
    z0tm = A.at(pa.off["kendm"], "z0tm", [128, 1024], F32)
    tkv = pa("tkv", [128, 8, 2, 16], F32)
    tki = pa("tki", [128, 8, 2, 16], U32)
    tkif = pa("tkif", [128, 8, 2, 16], F32)
    tmp128 = pa("tmp128", [128, 128], F32)
    cand = pa("cand", [128, 16, 16], F32)
    cand2 = pa("cand2", [128, 16, 16], F32)
    bv = pa("bv", [128, 8, 16], F32)
    posu = pa("posu", [128, 8, 16], U32)
    posr = pa("posr", [128, 8, 16], U32)
    posc = pa("posc", [128, 8, 16], U32)
    rf = pa("rf", [128, 8, 16], F32)
    cf = pa("cf", [128, 8, 16], F32)
    i12g = pa("i12g", [128, 3, 128], F32)
    i12gb = pa("i12gb", [128, 3, 128], BF16)
    zsum = pa("zsum", [128, 8], F32)
    tmp128b = A.at(pa.off["tkif"], "tmp128b", [128, 128], F32)
    candb = A.at(pa.off["rf"], "candb", [128, 16, 16], F32)
    cand2b = A.at(pa.off["posr"], "cand2b", [128, 16, 16], F32)
    assert pa.off["cf"] == pa.off["rf"] + 512 and pa.off["posc"] == pa.off["posr"] + 512
    bvs = pa("bvs", [128, 8, 16], F32)
    ub2 = pa("ub2", [128, 2, 1024], BF16)
    uT2 = pa("uT2", [128, 2, 8, 128], BF16)
    cmatA = A.at(pa.off["wslot0"], "cmatA", [128, 2324], F32)
    skst = A.at(pa.off["E1"], "skst", [128, 16, 128], F32)
    skb = A.at(pa.off["qdT"], "skb", [128, 16, 128], BF16)
    wst = A.at(pa.off["E2"], "wst", [128, 2048], F32)

    CM = ['ws0', 'ws1']
    dma(cmatA[:, :], cmat_d, P0RES, CM + P0RES, 'cmatA')
    cp('dve', maskTb[:, :], cmatA[:, M_MASK:M_MASK + 128], CM, ['maskTb'])
    cp('dve', resetm[:, :], cmatA[:, M_RESET:M_RESET + 256], CM, ['resetm'])
    cp('dve', bands[:, :, :], cmatA[:, M_BANDS:M_BANDS + 1536].rearrange("p (a b) -> p a b", a=12), CM, ['bands'])
    cp('dve', onesb[:, :], cmatA[:, M_ONES:M_ONES + 128], CM, ['onesb'])
    dma(wst[:, 0:1024], wpool_d.rearrange("p a b -> p (a b)"), P0RES, ['E2'], 'wst')
    cp('dve', pwb[:, :, :], wst[:, 0:1024].rearrange("p (a b) -> p a b", a=4), ['E2'], ['pwb'])
    dma(wst[:, :], wpp_d.rearrange("p a b -> p (a b)"), ['E2'], ['E2'], 'wst')
    cp('dve', wppb[:, :, :], wst[:, :].rearrange("p (a b) -> p a b", a=2), ['E2'], ['wppb'])
    dma(skst[:, :, :], sk_d, P0RES, ['E1'], 'skst')
    cp('dve', skb[:, :, :], skst[:, :, :], ['E1'], ['qdT'])
    for q4 in range(4):
        bk = nb()
        for j in range(4):
            hp = q4 * 4 + j
            tr(bankb(bk)[:, j * 128:(j + 1) * 128], skb[:, hp, :], identb[:, :], ['qdT', 'identb'], [PB(bk)])
        cp('act', skT[:, q4 * 4:(q4 + 1) * 4, :], bankb(bk)[:, 0:512].rearrange("p (a b) -> p a b", a=4), [PB(bk)], ['skT'])
    add('dve', lambda e: e.memset(S32[:, :, :], 0.0), w=['S32'])
    add('pool', lambda e: e.memset(Sbs[:, 0, :, :], 0.0), w=['Sbs0'])
    add('pool', lambda e: e.memset(vp[:, 0, :], 0.0), w=['vp'])


    ust = dict(loaded=-1, done=-1)

    def u_step():
        if ust['done'] < ust['loaded']:
            g = ust['loaded']
            for b in range(2):
                bk = nb()
                for k in range(8):
                    tr(bankb(bk)[:, k * 128:(k + 1) * 128], ub2[:, b, k * 128:(k + 1) * 128], identb[:, :], ['ub2', 'identb'], [PB(bk)])
                cp('act', uT2[:, b, :, :], bankb(bk).rearrange("p (k e) -> p k e", k=8), [PB(bk)], ['uT2'])
            dma(uT_d[g * 2:(g + 1) * 2].rearrange("b p k e -> p b k e"), uT2[:, :, :, :], ['uT2'], ['uT_s'], 'uT_s')
            ust['done'] = g
        if ust['loaded'] < 63:
            g = ust['loaded'] + 1
            add('pool', lambda e, g=g: e.dma_start(out=ub2[:, :, :], in_=u_d[g * 256:(g + 1) * 256, :].rearrange("(b p) d -> p b d", p=128)),
                w=['ub2'], dsem=('ub2', 'slot'))
            ust['loaded'] = g

    vst = [0]

    def v_step(n=2):
        for _ in range(n):
            if vst[0] < 32:
                g = vst[0]
                add('pool', lambda e, g=g: e.dma_start(out=vs_d[g * 4:(g + 1) * 4].rearrange("b e d -> (b e) d"), in_=v_d[g * 512:(g + 1) * 512, :]),
                    w=['v_s'], dsem=('vcast', 'slot'))
                vst[0] += 1

    wuse = [0]

    def wload(c):
        s = wuse[0] % 2
        wuse[0] += 1
        dma(wslot[s][:, :, :], wscr_d[c], ['wscr%d' % c], ['ws%d' % s], 'ws%d' % s)
        return wslot[s], 'ws%d' % s

    def proj_fm(c, rhsT, rres, evac):
        wt, wr = wload(c)
        for half in range(2):
            bk = nb()
            for j in range(2):
                m = half * 2 + j
                for k in range(8):
                    mm(bank(bk)[:, j * 256:(j + 1) * 256], wt[:, k, m * 128:(m + 1) * 128], rhsT[:, k, :], k == 0, k == 7, [wr] + rres, [PB(bk)])
            evac(bk, half)

    def b3(bk):
        return bank(bk).rearrange("p (a b) -> p a b", a=2)

    RMS_EPS = 1e-6
    LN_EPS = 1e-5
    def gen_X1(t):
        T0 = t * 256
        u_step()
        v_step(2)
        for sub in range(2):
            xs = xt[0]
            xr = 'xt0'
            r0 = T0 + sub * 128
            dma(xs[:, :], x_d[r0:r0 + 128, :], [], [xr], xr)
            add('dve', lambda e, xs=xs: e.bn_stats(out=lnst[:, 0, :], in_=xs[:, 0:512]), r=[xr], w=['lnst'])
            add('dve', lambda e, xs=xs: e.bn_stats(out=lnst[:, 1, :], in_=xs[:, 512:1024]), r=[xr], w=['lnst1'])
            add('dve', lambda e: e.bn_aggr(out=lnmv[:, :], in_=lnst[:, :, :].rearrange("p a b -> p (a b)")), r=['lnst', 'lnst1'], w=['lnmv'])
            act(lnr[:, :], lnmv[:, 1:2], AF.Sqrt, ['lnmv'], ['lnr'], bias=LN_EPS)
            add('dve', lambda e: e.reciprocal(out=lnr[:, :], in_=lnr[:, :]), r=['lnr'], w=['lnr'])
            ts('dve', xs[:, :], xs[:, :], lnmv[:, 0:1], lnr[:, 0:1], ALU.subtract, ALU.mult, [xr, 'lnmv', 'lnr'], [xr])
            bk = nb(2)
            for k in range(8):
                tr(ps[:, bk * 512 + k * 128: bk * 512 + (k + 1) * 128], xs[:, k * 128:(k + 1) * 128], identf[:, :], [xr, 'identf'], [PB(bk), PB(bk + 1)])
            for k in range(8):
                act(hT32[:, k, sub * 128:(sub + 1) * 128], ps[:, bk * 512 + k * 128: bk * 512 + (k + 1) * 128], AF.Identity,
                    [PB(bk), PB(bk + 1), 'cvec'], ['hT32'], scale=cvec[:, C_G0 + k:C_G0 + k + 1], bias=cvec[:, C_B0 + k:C_B0 + k + 1])
            yield
        cp('act', hTb[:, :, :], hT32[:, :, :], ['hT32'], ['hTb'])
        yield

        def ev_act(dst, dres, func, scale=1.0):
            def f(bk, half, c0):
                act(dst[:, c0 + half * 2:c0 + half * 2 + 2, :], b3(bk), func, [PB(bk)], [dres], scale=scale)
            return f
        for c in (0, 1):
            proj_fm(c, hTb, ['hTb'], lambda bk, half, c=c: ev_act(qsT, 'qsT', AF.Silu)(bk, half, c * 4))
            yield
        for c in (6, 7):
            proj_fm(c, hTb, ['hTb'], lambda bk, half, c=c: ev_act(sgT, 'sgT', AF.Silu)(bk, half, (c - 6) * 4))
            yield
        for c in (2, 3):
            proj_fm(c, hTb, ['hTb'], lambda bk, half, c=c: ev_act(E2, 'E2', AF.Sigmoid, -1.0)(bk, half, (c - 2) * 4))
            yield
        for c in (9, 10):
            proj_fm(c, hTb, ['hTb'], lambda bk, half, c=c: ev_act(gaT, 'gaT', AF.Sigmoid)(bk, half, (c - 9) * 4))
            yield
        for c in (11, 12):
            proj_fm(c, hTb, ['hTb'], lambda bk, half, c=c: ev_act(gbT, 'gbT', AF.Sigmoid)(bk, half, (c - 11) * 4))
            yield
        for c in (4, 5, 8):
            wt, wr = wload(c)
            for sub in range(2):
                bk = nb()
                for k in range(8):
                    mm(bank(bk), hTb[:, k, sub * 128:(sub + 1) * 128], wt[:, k, :], k == 0, k == 7, [wr, 'hTb'], [PB(bk)])
                if c == 8:
                    cp('act', vp[:, 1 + sub, :], bank(bk), [PB(bk)], ['vp'])
                else:
                    cp('act', vtm[:, sub, (c - 4) * 512:(c - 3) * 512], bank(bk), [PB(bk)], ['vtm'])
            yield

    def gen_A3(t):
        T0 = t * 256
        yield 'WAIT_SC1'
        u_step()
        for h in range(8):
            act(E1[:, h, :], E2[:, h, :], AF.Ln, ['E2', 'cvec'], ['E1'], scale=cvec[:, C_NOML + h:C_NOML + h + 1], bias=1.0)
        tt('pool', kiT[:, :, :], E2[:, :, :], cvec[:, C_OML:C_OML + 8].unsqueeze(2).to_broadcast([128, 8, 256]), ALU.mult, ['E2', 'cvec'], ['kiT'])
        for h in range(8):
            add('dve', lambda e, h=h: e.tensor_tensor_scan(out=bhT[:, h, :], data0=resetm[:, :], data1=E1[:, h, :], initial=0.0,
                                                             op0=ALU.mult, op1=ALU.add), r=['E1', 'resetm'], w=['bhT'])
        yield
        NCT = 256 // HG_CH
        bh3 = bhT[:, :, :].rearrange("p h (c s) -> p (h c) s", s=HG_CH)
        tt('dve', E2[:, :, :].rearrange("p h (c s) -> p (h c) s", s=HG_CH), bh3[:, :, HG_CH - 1:HG_CH].to_broadcast([128, 8 * NCT, HG_CH]), bh3, ALU.subtract,
           ['bhT', 'kiT'], ['E2'])
        act(E2[:, :, :], E2[:, :, :], AF.Exp, ['E2'], ['E2'])
        tt('dve', kendT[:, :, :], kiT[:, :, :], E2[:, :, :], ALU.mult, ['kiT', 'E2'], ['kendT'])
        act(E1[:, :, :], bhT[:, :, :], AF.Exp, ['bhT'], ['E1'])
        stt(qdT[:, :, :], qsT[:, :, :], 128.0 ** -0.5, E1[:, :, :], ALU.mult, ALU.mult, ['qsT', 'E1'], ['qdT'])
        yield
        act(E2[:, :, :], bhT[:, :, :], AF.Exp, ['bhT', 'kendT'], ['E2'], scale=-1.0)
        tt('pool', kiT[:, :, :], kiT[:, :, :], E2[:, :, :], ALU.mult, ['kiT', 'E2'], ['kiT'])
        act(dec[:, :, :].rearrange("p h c -> p (h c)"), bh3[:, :, HG_CH - 1], AF.Exp, ['bhT'], ['dec'])
        yield

    def gen_Sc(t):
        T0 = t * 256
        for sub in range(2):
            gsub = t * 2 + sub
            par = gsub % 2
            bk = nb()
            for h in range(8):
                tr(bankb(bk)[:, h * 128:(h + 1) * 128], kendT[:, h, sub * 128:(sub + 1) * 128], identb[:, :], ['kendT', 'identb'], [PB(bk)])
            NCS = 128 // HG_CH
            for cc in range(NCS):
                act(kendm[:, cc, :], bankb(bk), AF.Copy, [PB(bk), 'cmask4'], ['kendm'], scale=cmask4[:, cc:cc + 1])
            for cc in range(NCS):
                db = nb(2)
                for h in range(8):
                    mm(ps[:, db * 512 + h * 128: db * 512 + (h + 1) * 128], kendm[:, cc, h * 128:(h + 1) * 128], vtm[:, sub, h * 128:(h + 1) * 128],
                       True, True, ['kendm', 'vtm'], [PB(db), PB(db + 1)])
                gc = sub * NCS + cc
                tt('dve', S32[:, :, :], S32[:, :, :], dec[:, :, gc:gc + 1].to_broadcast([128, 8, 128]), ALU.mult, ['S32', 'dec'], ['S32'])
                tt('dve', S32[:, :, :], S32[:, :, :], ps[:, db * 512:(db + 2) * 512].rearrange("p (h e) -> p h e", h=8), ALU.add,
                   ['S32', PB(db), PB(db + 1)], ['S32'])
                nslot = cc + 1 if cc < NCS - 1 else (4 if par == 0 else 0)
                cp('act', Sbs[:, nslot, :, :], S32[:, :, :], ['S32'], ['Sbs%d' % nslot])
            yield
            sb_ = [nb(), nb()]
            for h in range(8):
                mm(bank(sb_[h // 4])[:, (h % 4) * 128:(h % 4 + 1) * 128], kiT[:, h, sub * 128:(sub + 1) * 128], qdT[:, h, sub * 128:(sub + 1) * 128],
                   True, True, ['kiT', 'qdT'], [PB(sb_[h // 4])])
            for g2 in range(2):
                tt('dve', scm[:, g2 * 4:(g2 + 1) * 4, :], bank(sb_[g2]).rearrange("p (a b) -> p a b", a=4),
                   maskTb[:, :].unsqueeze(1).to_broadcast([128, 4, 128]), ALU.mult, [PB(sb_[g2]), 'maskTb'], ['scm'])
            ob = [nb(), nb()]
            for h in range(8):
                reg = bank(ob[h // 4])[:, (h % 4) * 128:(h % 4 + 1) * 128]
                mm(reg, vtm[:, sub, h * 128:(h + 1) * 128], scm[:, h, :], True, False, ['vtm', 'scm'], [PB(ob[h // 4])])
                for cc in range(NCS):
                    slot = 4 if (par == 1 and cc == 0) else cc
                    mm(reg[:, cc * HG_CH:(cc + 1) * HG_CH], Sbs[:, slot, h, :], qdT[:, h, sub * 128 + cc * HG_CH: sub * 128 + (cc + 1) * HG_CH], False, cc == NCS - 1,
                       ['Sbs%d' % slot, 'qdT'], [PB(ob[h // 4])])
            for g2 in range(2):
                act(sqb[:, g2 * 4:(g2 + 1) * 4, :], bank(ob[g2]).rearrange("p (a b) -> p a b", a=4), AF.Square, [PB(ob[g2])], ['sqb'])
                mb = nb()
                mm(bank(mb), onesb[:, :], sqb[:, g2 * 4:(g2 + 1) * 4, :].rearrange("p a b -> p (a b)"), True, True, ['onesb', 'sqb'], [PB(mb)])
                act(rstd[:, g2 * 4:(g2 + 1) * 4, :], bank(mb).rearrange("p (a b) -> p a b", a=4), AF.Sqrt, [PB(mb)], ['rstd'], bias=RMS_EPS)
            add('dve', lambda e: e.reciprocal(out=rstd[:, :, :], in_=rstd[:, :, :]), r=['rstd'], w=['rstd'])
            for g2 in range(2):
                tt('dve', on[:, g2 * 4:(g2 + 1) * 4, :], bank(ob[g2]).rearrange("p (a b) -> p a b", a=4), rstd[:, g2 * 4:(g2 + 1) * 4, :], ALU.mult,
                   [PB(ob[g2]), 'rstd'], ['on'])
            stt(onT[:, :, sub * 128:(sub + 1) * 128], on[:, :, :], cvec[:, C_NG:C_NG + 1], sgT[:, :, sub * 128:(sub + 1) * 128], ALU.mult, ALU.mult,
                ['on', 'cvec', 'sgT'], ['onT'])
            yield

        u_step()
        for g2 in range(2):
            bk = nb()
            for gg in range(2):
                g = g2 * 2 + gg
                for sub in range(2):
                    first = (t == 0 and sub == 0)
                    reg = bank(bk)[:, gg * 256 + sub * 128: gg * 256 + (sub + 1) * 128]
                    mm(reg, vp[:, 1 + sub, g * 128:(g + 1) * 128], bands[:, (8 + g) if first else g, :], True, first, ['vp', 'bands'], [PB(bk)])
                    if not first:
                        mm(reg, vp[:, sub, g * 128:(g + 1) * 128], bands[:, 4 + g, :], False, True, ['vp', 'bands'], [PB(bk)])
            cp('act', plT[:, g2 * 2:g2 * 2 + 2, :], b3(bk), [PB(bk)], ['plT'])
        cp('pool', vp[:, 0, :], vp[:, 2, :], ['vp', 'plT'], ['vp'])
        yield
        wa = [None, None]
        for q in range(4):
            if q % 2 == 0:
                wa = wload(13 + q // 2)
            wt, wr = wa
            ya = nb()
            for j in range(2):
                m = (q % 2) * 2 + j
                for k in range(8):
                    mm(bank(ya)[:, j * 256:(j + 1) * 256], wt[:, k, m * 128:(m + 1) * 128], onT[:, k, :], k == 0, k == 7, [wr, 'onT'], [PB(ya)])
            yb = nb()
            for j in range(2):
                mm(bank(yb)[:, j * 256:(j + 1) * 256], pwb[:, q, j * 128:(j + 1) * 128], plT[:, q, :], True, True, ['pwb', 'plT'], [PB(yb)])
            tt('dve', E1[:, 2 * q:2 * q + 2, :], b3(ya), gaT[:, 2 * q:2 * q + 2, :], ALU.mult, [PB(ya), 'gaT'], ['E1'])
            for j in range(2):
                m = 2 * q + j
                stt(E2[:, m, :], bank(yb)[:, j * 256:(j + 1) * 256], cvec[:, C_PSC + m:C_PSC + m + 1], gbT[:, m, :], ALU.mult, ALU.mult,
                    [PB(yb), 'cvec', 'gbT'], ['E2'])
            yield
        tt('pool', mixT[:, :, :], E1[:, :, :], E2[:, :, :], ALU.add, ['E1', 'E2'], ['kiT'])

        for c in (15, 16):
            def evz(bk, half, c=c):
                m0 = (c - 15) * 4 + half * 2
                stt(hT32[:, m0:m0 + 2, :], hT32[:, m0:m0 + 2, :], ALPHA, b3(bk), ALU.mult, ALU.add, ['hT32', PB(bk)], ['hT32'])
            proj_fm(c, mixT, ['kiT'], evz)
            yield
        cp('act', zb[:, :, :], hT32[:, :, :], ['hT32'], ['kendT'])
        mb = nb()
        for k in range(8):
            mm(bank(mb)[:, 0:256], onesb[:, :], zb[:, k, :], k == 0, k == 7, ['onesb', 'kendT'], [PB(mb)])
        stt(hT32[:, :, :], bank(mb)[:, 0:256].unsqueeze(1).to_broadcast([128, 8, 256]), -0.125, hT32[:, :, :], ALU.mult, ALU.add,
            [PB(mb), 'hT32'], ['hT32'])
        act(zb[:, :, :], hT32[:, :, :], AF.Square, ['hT32'], ['kendT'])
        vb_ = nb()
        for k in range(8):
            mm(bank(vb_)[:, 0:256], onesb[:, :], zb[:, k, :], k == 0, k == 7, ['onesb', 'kendT'], [PB(vb_)])
        act(lnrs[:, :], bank(vb_)[:, 0:256], AF.Sqrt, [PB(vb_)], ['lnrs'], scale=0.125, bias=LN_EPS)
        add('dve', lambda e: e.reciprocal(out=lnrs[:, :], in_=lnrs[:, :]), r=['lnrs'], w=['lnrs'])
        tt('pool', hT32[:, :, :], hT32[:, :, :], lnrs[:, :].unsqueeze(1).to_broadcast([128, 8, 256]), ALU.mult, ['hT32', 'lnrs'], ['hT32'])
        for k in range(8):
            act(hT32[:, k, :], hT32[:, k, :], AF.Identity, ['hT32', 'cvec'], ['hT32'],
                scale=cvec[:, C_G1 + k:C_G1 + k + 1], bias=cvec[:, C_B1 + k:C_B1 + k + 1])
        cp('act', hTb[:, :, :], hT32[:, :, :], ['hT32'], ['hTb'])
        dma(h1T_d[:, :, T0:T0 + 256], hTb[:, :, :], ['hTb'], ['h1T_s'], 'h1T_s')
        yield

    def gen_A7(t):
        T0 = t * 256
        u_step()
        dma(pld[:, :, :], p_d[T0:T0 + 256, :].rearrange("(s p) c -> p s c", p=128), [], ['pld'], 'pld')
        cp('pool', pldb[:, :, :], pld[:, :, :], ['pld'], ['pldb'])
        bk = nb()
        for kk in range(2):
            for sub in range(2):
                tr(bankb(bk)[:, (kk * 2 + sub) * 128:(kk * 2 + sub + 1) * 128], pldb[:, sub, kk * 128:(kk + 1) * 128], identb[:, :],
                   ['pldb', 'identb'], [PB(bk)])
        cp('act', pT[:, :, :], bankb(bk)[:, 0:512].rearrange("p (k t) -> p k t", k=2), [PB(bk)], ['pT'])
        yield
        for q in range(4):
            bk = nb()
            for j in range(2):
                m = 2 * q + j
                for kk in range(2):
                    mm(bank(bk)[:, j * 256:(j + 1) * 256], wppb[:, kk, m * 128:(m + 1) * 128], pT[:, kk, :], kk == 0, kk == 1, ['wppb', 'pT'], [PB(bk)])
            cp('act', E2[:, 2 * q:2 * q + 2, :], b3(bk), [PB(bk), 'kiT'], ['E2'])
            yield
        for c in (21, 22):
            proj_fm(c, hTb, ['hTb'], lambda bk, half, c=c: act(E1[:, (c - 21) * 4 + half * 2:(c - 21) * 4 + half * 2 + 2, :], b3(bk), AF.Sigmoid,
                                                                 [PB(bk), 'kiT'], ['E1']))
            yield
        tt('pool', E1[:, :, :], E1[:, :, :], E2[:, :, :], ALU.mult, ['E1', 'E2'], ['E1'])
        stt(E1[:, :, :], hT32[:, :, :], ALPHA, E1[:, :, :], ALU.mult, ALU.add, ['hT32', 'E1'], ['E1'])
        for sub in range(2):
            bk = nb(2)
            for k in range(8):
                tr(ps[:, bk * 512 + k * 128: bk * 512 + (k + 1) * 128], E1[:, k, sub * 128:(sub + 1) * 128], identf[:, :], ['E1', 'identf'],
                   [PB(bk), PB(bk + 1)])
            cp('act', z0tm[:, :], ps[:, bk * 512:(bk + 2) * 512], [PB(bk), PB(bk + 1)], ['kendm'])
            dma(z0_d[T0 + sub * 128:T0 + (sub + 1) * 128, :], z0tm[:, :], ['kendm'], ['z0_s'], 'z0_s')
            yield

    def gen_Sd(t):
        T0 = t * 256
        u_step()
        for c in (17, 18, 19, 20):
            proj_fm(c, hTb, ['hTb'], lambda bk, half, c=c: cp('act', qpT[:, (c - 17) * 4 + half * 2:(c - 17) * 4 + half * 2 + 2, :], b3(bk),
                                                                [PB(bk)], ['bhT']))
            yield
    def gen_chain(t):
        T0 = t * 256
        for sub in range(2):
            for q4 in range(4):
                bk = nb()
                for j in range(4):
                    hp = q4 * 4 + j
                    mm(bank(bk)[:, j * 128:(j + 1) * 128], qpT[:, hp, sub * 128:(sub + 1) * 128], skT[:, hp, :], True, True, ['bhT', 'skT'], [PB(bk)])
                cp('act', sc[:, q4 * 4:(q4 + 1) * 4, :], bank(bk).rearrange("p (a b) -> p a b", a=4), [PB(bk)], ['sc'])
            if sub == 1:
                yield 'SC1'
            def head_ops(h, tmpb, cnd, cnd2, r_tmp, r_c, r_c2):
                ops_ = []
                TK, TKI, BV, PU = 'tk%d' % h, 'tki%d' % h, 'bv%d' % h, 'posu%d' % h
                for p_ in range(2):
                    hp = 2 * h + p_
                    ops_.append(lambda h=h, p_=p_, hp=hp: add('dve', lambda e: e.max(out=tkv[:, h, p_, 0:8], in_=sc[:, hp, :]), r=['sc'], w=[TK]))
                    ops_.append(lambda h=h, p_=p_, hp=hp: add('dve', lambda e: e.match_replace(out=tmpb[:, :], in_to_replace=tkv[:, h, p_, 0:8],
                                                                                             in_values=sc[:, hp, :], imm_value=-1e30), r=['sc', TK], w=r_tmp))
                    ops_.append(lambda h=h, p_=p_: add('dve', lambda e: e.max(out=tkv[:, h, p_, 8:16], in_=tmpb[:, :]), r=r_tmp, w=[TK]))
                    ops_.append(lambda h=h, p_=p_, hp=hp: add('dve', lambda e: e.max_index(out=tki[:, h, p_, 0:8], in_max=tkv[:, h, p_, 0:8],
                                                                                         in_values=sc[:, hp, :]), r=['sc', TK], w=[TKI]))
                    ops_.append(lambda h=h, p_=p_, hp=hp: add('dve', lambda e: e.max_index(out=tki[:, h, p_, 8:16], in_max=tkv[:, h, p_, 8:16],
                                                                                         in_values=sc[:, hp, :]), r=['sc', TK], w=[TKI]))
                cflat = cnd[:, :, :].rearrange("p a b -> p (a b)")
                c2flat = cnd2[:, :, :].rearrange("p a b -> p (a b)")
                ops_.append(lambda h=h: tt('dve', cnd[:, :, :], tkv[:, h, 0, :].unsqueeze(2).to_broadcast([128, 16, 16]),
                                           tkv[:, h, 1, :].unsqueeze(1).to_broadcast([128, 16, 16]), ALU.add, [TK], r_c))
                ops_.append(lambda h=h: add('dve', lambda e: e.max(out=bv[:, h, 0:8], in_=cflat), r=r_c, w=[BV]))
                ops_.append(lambda h=h: add('dve', lambda e: e.match_replace(out=c2flat, in_to_replace=bv[:, h, 0:8], in_values=cflat, imm_value=-1e30),
                                            r=r_c + [BV], w=r_c2))
                ops_.append(lambda h=h: add('dve', lambda e: e.max(out=bv[:, h, 8:16], in_=c2flat), r=r_c2, w=[BV]))
                ops_.append(lambda h=h: add('dve', lambda e: e.max_index(out=posu[:, h, 0:8], in_max=bv[:, h, 0:8], in_values=cflat), r=r_c + [BV], w=[PU]))
                ops_.append(lambda h=h: add('dve', lambda e: e.max_index(out=posu[:, h, 8:16], in_max=bv[:, h, 8:16], in_values=cflat), r=r_c + [BV], w=[PU]))
                return ops_
            for hh in range(0, 8, 2):
                oa = head_ops(hh, tmp128, cand, cand2, ['tmp128'], ['cand'], ['cand2'])
                ob_ = head_ops(hh + 1, tmp128b, candb, cand2b, ['tkif'], ['rf', 'cf'], ['posr', 'posc'])
                for iz, (fa, fb) in enumerate(zip(oa, ob_)):
                    fa()
                    fb()
                    if iz % 4 == 3:
                        yield
            TKS = ['tk%d' % h for h in range(8)]
            TKIS = ['tki%d' % h for h in range(8)]
            BVS = ['bv%d' % h for h in range(8)]
            PUS = ['posu%d' % h for h in range(8)]
            ts('dve', posr[:, :, :], posu[:, :, :], 4, None, ALU.logical_shift_right, ALU.bypass, PUS, ['posr'])
            ts('dve', posc[:, :, :], posu[:, :, :], 15, None, ALU.bitwise_and, ALU.bypass, PUS, ['posc'])
            cp('dve', rf[:, :, :], posr[:, :, :], ['posr'], ['rf'])
            cp('dve', cf[:, :, :], posc[:, :, :], ['posc'], ['cf'])
            cp('dve', tkif[:, :, :, :], tki[:, :, :, :], TKIS, ['tkif'])
            yield
            EQ = eqb[:, :, :, :]
            io4 = iota16[:, :].unsqueeze(1).unsqueeze(1).to_broadcast([128, 8, 16, 16])
            for which, src in ((0, rf), (1, cf)):
                tt('dve', EQ, src[:, :, :].unsqueeze(3).to_broadcast([128, 8, 16, 16]), io4, ALU.is_equal, ['rf', 'cf', 'iota16'], ['eqb'])
                tt('dve', EQ, EQ, tkif[:, :, which, :].unsqueeze(2).to_broadcast([128, 8, 16, 16]), ALU.mult, ['eqb', 'tkif'], ['eqb'])
                add('dve', lambda e, which=which, EQ=EQ: e.tensor_reduce(out=i12g[:, which, :].rearrange("p (h k) -> p h k", h=8), in_=EQ,
                                                                          axis=AX.X, op=ALU.add), r=['eqb'], w=['i12g'])
            tt('dve', bvs[:, :, :], bv[:, :, :], bv[:, :, 0:1].to_broadcast([128, 8, 16]), ALU.subtract, BVS + PUS, ['bvs'])
            act(bvs[:, :, :], bvs[:, :, :], AF.Exp, ['bvs'], ['bvs'])
            add('dve', lambda e: e.tensor_reduce(out=zsum[:, :], in_=bvs[:, :, :], axis=AX.X, op=ALU.add), r=['bvs'], w=['zsum'])
            add('dve', lambda e: e.reciprocal(out=zsum[:, :], in_=zsum[:, :]), r=['zsum'], w=['zsum'])
            tt('dve', i12g[:, 2, :].rearrange("p (h k) -> p h k", h=8), bvs[:, :, :], zsum[:, :].unsqueeze(2).to_broadcast([128, 8, 16]), ALU.mult,
               ['bvs', 'zsum', 'i12g'], ['i12g'])
            yield
            cp('pool', i12gb[:, :, :], i12g[:, :, :], ['i12g'], ['i12gb'])
            bk = nb()
            for j in range(3):
                tr(bankb(bk)[:, j * 128:(j + 1) * 128], i12gb[:, j, :], identb[:, :], ['i12gb', 'identb'], [PB(bk)])
            tk0 = T0 + sub * 128
            for j, dst in enumerate((I1T, I2T, GT)):
                cp('act', dst[:, tk0:tk0 + 128], bankb(bk)[:, j * 128:(j + 1) * 128], [PB(bk)], ['idxT'])


    def drain(g):
        for _ in g:
            pass

    def seq(*gens):
        for g in gens:
            if g is not None:
                yield from g

    def merge(g1, g2):
        flags = set()
        nstep = [0]
        a, b, blocked = True, True, False
        while a or b:
            if a:
                try:
                    v = next(g1)
                    if v:
                        flags.add(v)
                except StopIteration:
                    a = False
                    flags.add('SC1')
            nstep[0] += 1
            for _rep in range(2 if nstep[0] % 3 == 0 else 1):
                if b and not (blocked and 'SC1' not in flags):
                    blocked = False
                    try:
                        v = next(g2)
                        if v == 'WAIT_SC1' and 'SC1' not in flags:
                            blocked = True
                    except StopIteration:
                        b = False

    drain(gen_X1(0))
    drain(gen_A3(0))
    drain(gen_Sc(0))
    drain(gen_Sd(0))
    for t in range(NT):
        if t + 1 < NT:
            merge(gen_chain(t), seq(gen_A7(t), gen_X1(t + 1), gen_A3(t + 1), gen_Sc(t + 1), gen_Sd(t + 1)))
        else:
            merge(gen_chain(t), gen_A7(t))
    while ust['done'] < 63:
        u_step()
    v_step(32)

    PARES = ['tk%d' % i for i in range(8)] + ['tki%d' % i for i in range(8)] + ['bv%d' % i for i in range(8)] + ['posu%d' % i for i in range(8)] + ['maskTb', 'resetm', 'bands', 'onesb', 'pwb', 'wppb', 'skT', 'ws0', 'ws1', 'ws2', 'xt0', 'sc', 'eqb', 'hT32', 'hTb', 'qsT',
             'vtm', 'sgT', 'vp', 'gaT', 'gbT', 'E1', 'E2', 'bhT', 'qdT', 'kiT', 'kendT', 'kendm', 'scm', 'S32', 'dec', 'sqb', 'rstd', 'on', 'onT',
             'plT', 'kiT', 'kendT', 'lnrs', 'pld', 'pldb', 'pT', 'kendm', 'tkif', 'tmp128', 'cand', 'cand2', 'posr', 'posc',
             'rf', 'cf', 'i12g', 'i12gb', 'zsum', 'bvs', 'ub2', 'uT2', 'lnst', 'lnst1', 'lnmv', 'lnr'] + ['Sbs%d' % i for i in range(8)]
    pbm = Bump(A, P_END, SB_END)
    G = pbm("G", [128, 256, 128], BF16)
    Pm = [pbm("Pm%d" % i, [128, 32, 128], BF16) for i in range(2)]
    Qm = [pbm("Qm%d" % i, [128, 32, 128], BF16) for i in range(2)]
    h1t = [pbm("h1t%d" % i, [128, 8, 256], BF16) for i in range(2)]
    z0l = pbm("z0l", [128, 2, 1024], F32)
    bc2 = pbm("bc2", [128, 2048], F32)
    ut = [pbm("ut%d" % i, [128, 4, 8, 128], BF16) for i in range(3)]
    vt = [pbm("vt%d" % i, [128, 4, 1024], BF16) for i in range(3)]
    gel = [pbm("gel%d" % i, [128, 256], BF16) for i in range(4)]
    GA = [pbm("GA%d" % i, [128, 256], BF16) for i in range(4)]
    zfin = [pbm("zfin%d" % i, [128, 1024], F32) for i in range(2)]
    st2 = pbm("st2", [128, 2, 6], F32)
    mv2 = pbm("mv2", [128, 2], F32)
    r2 = pbm("r2", [128, 1], F32)

    add('pool', lambda e: e.memset(r2[:, :], 0.0), r=PARES, w=PARES + ['PBGO'])
    dma(bc2[:, :], bc2_d, ['PBGO'], ['bc2'], 'bc2')
    ug = [0]
    for t in range(NT):
        T0 = t * 256
        hs = t % 2
        dma(h1t[hs][:, :, :], h1T_d[:, :, T0:T0 + 256], ['h1T_s', 'PBGO'], ['h1t%d' % hs], 'h1t%d' % hs)
        dma(z0l[:, :, :], z0_d[T0:T0 + 256, :].rearrange("(s p) d -> p s d", p=128), ['z0_s', 'PBGO'], ['z0l'], 'z0l')
        for sbi in range(8):
            s = sbi % 2
            tk0 = T0 + sbi * 32
            iob = iotab[:, :].unsqueeze(1).to_broadcast([128, 32, 128])
            for tl in range(32):
                ts('dve', Pm[s][:, tl, :], iotab[:, :], I1T[:, tk0 + tl:tk0 + tl + 1], GT[:, tk0 + tl:tk0 + tl + 1], ALU.is_equal, ALU.mult,
                   ['idxT', 'iotab', 'PBGO'], ['Pm%d_%d' % (s, tl)])
            tt('dve', Qm[s][:, :, :], I2T[:, tk0:tk0 + 32].unsqueeze(2).to_broadcast([128, 32, 128]), iob, ALU.is_equal,
               ['idxT', 'iotab', 'PBGO'], ['Qm%d' % s])
            for tg in range(8):
                bk = 6 + (tg % 2)
                for j in range(4):
                    tl = tg * 4 + j
                    mm(bank(bk)[:, j * 128:(j + 1) * 128], Qm[s][:, tl, :], Pm[s][:, tl, :], True, True, ['Qm%d' % s, 'Pm%d_%d' % (s, tl)], [PB(bk)])
                cp('act', G[:, sbi * 32 + tg * 4: sbi * 32 + tg * 4 + 4, :], bank(bk).rearrange("p (a b) -> p a b", a=4), [PB(bk), 'PBGO'], ['G'])
        slots = {}

        def issue_A(i1):
            g, b = divmod(i1, 4)
            if b == 0:
                sl = ug[0] % 3
                ug[0] += 1
                slots[g] = sl
                dma(ut[sl][:, :, :, :], uT_d[g * 4:(g + 1) * 4].rearrange("b p k e -> p b k e"), ['uT_s', 'PBGO'], ['ut%d' % sl], 'ut%d' % sl)
                dma(vt[sl][:, :, :], vs_d[g * 4:(g + 1) * 4].rearrange("b e d -> e b d"), ['v_s', 'PBGO'], ['vt%d' % sl], 'vt%d' % sl)
            sl = slots[g]
            ab = 4 + (i1 % 4)
            for k in range(8):
                mm(bank(ab)[:, 0:256], ut[sl][:, b, k, :], h1t[hs][:, k, :], k == 0, k == 7, ['ut%d' % sl, 'h1t%d' % hs], [PB(ab)])
            gs = i1 % 4
            act(gel[gs][:, :], bank(ab)[:, 0:256], AF.Gelu, [PB(ab), 'PBGO'], ['gel%d' % gs])
            tt('dve', GA[gs][:, :], gel[gs][:, :], G[:, :, i1], ALU.mult, ['gel%d' % gs, 'G'], ['GA%d' % gs])

        def issue_out(i1):
            g, b = divmod(i1, 4)
            sl = slots[g]
            gs = i1 % 4
            for sub in range(2):
                for half in range(2):
                    fbk = sub * 2 + half
                    mm(bank(fbk), GA[gs][:, sub * 128:(sub + 1) * 128], vt[sl][:, b, half * 512:(half + 1) * 512], i1 == 0, i1 == 127,
                       ['GA%d' % gs, 'vt%d' % sl], [PB(fbk)])
        issue_A(0)
        issue_A(1)
        for i1 in range(128):
            if i1 + 2 < 128:
                issue_A(i1 + 2)
            issue_out(i1)
        for sub in range(2):
            zf = zfin[sub]
            zr = 'zfin%d' % sub
            tt('dve', zf[:, :], z0l[:, sub, :], ps[:, sub * 1024:(sub + 1) * 1024], ALU.add, ['z0l', PB(sub * 2), PB(sub * 2 + 1), 'PBGO'], [zr])
            add('dve', lambda e, zf=zf: e.bn_stats(out=st2[:, 0, :], in_=zf[:, 0:512]), r=[zr], w=['st2a'])
            add('dve', lambda e, zf=zf: e.bn_stats(out=st2[:, 1, :], in_=zf[:, 512:1024]), r=[zr], w=['st2b'])
            add('dve', lambda e: e.bn_aggr(out=mv2[:, :], in_=st2[:, :, :].rearrange("p a b -> p (a b)")), r=['st2a', 'st2b'], w=['mv2'])
            act(r2[:, :], mv2[:, 1:2], AF.Sqrt, ['mv2', 'PBGO'], ['r2'], bias=LN_EPS)
            add('dve', lambda e: e.reciprocal(out=r2[:, :], in_=r2[:, :]), r=['r2'], w=['r2'])
            ts('dve', zf[:, :], zf[:, :], mv2[:, 0:1], r2[:, 0:1], ALU.subtract, ALU.mult, [zr, 'mv2', 'r2'], [zr])
            tt('pool', zf[:, :], zf[:, :], bc2[:, 0:1024], ALU.mult, [zr, 'bc2'], [zr])
            tt('pool', zf[:, :], zf[:, :], bc2[:, 1024:2048], ALU.add, [zr, 'bc2'], [zr])
            dma(y_d[T0 + sub * 128:T0 + (sub + 1) * 128, :], zf[:, :], [zr], ['y%d' % sub], 'y%d' % sub)
    add('sp', lambda e: e.nop(), r=['y0', 'y1', 'h1T_s', 'z0_s'])
    S.emit(nc, st)
    st.close()
    return nc


def _consts():
    cm = np.zeros((128, 2324), np.float32)
    cm[:, 0:128] = np.eye(128, dtype=np.float32)
    s = np.arange(128)[:, None]
    c = np.arange(128)[None, :]
    cm[:, 128:256] = ((s // HG_CH == c // HG_CH) & (s <= c)).astype(np.float32)
    cm[:, 256:512] = (np.arange(256) % HG_CH != 0).astype(np.float32)[None, :]
    wins = (2, 4, 8, 16)
    for g, w in enumerate(wins):
        cur = np.where((s <= c) & (s > c - w), 1.0 / w, 0.0) - (s == c)
        prev = np.where(s > c + 128 - w, 1.0 / w, 0.0)
        cnt = np.minimum(c + 1, w)
        first = np.where((s <= c) & (s > c - w), 1.0 / cnt, 0.0) - (s == c)
        cm[:, 512 + g * 128: 512 + (g + 1) * 128] = cur
        cm[:, 512 + (4 + g) * 128: 512 + (5 + g) * 128] = prev
        cm[:, 512 + (8 + g) * 128: 512 + (9 + g) * 128] = first
    cm[:, 2048:2064] = np.arange(16, dtype=np.float32)[None, :]
    cm[:, 2064:2192] = np.arange(128, dtype=np.float32)[None, :]
    cm[:, 2192:2196] = (np.arange(128)[:, None] // HG_CH == np.arange(4)[None, :]).astype(np.float32)
    cm[:, 2196:2324] = 1.0 / 128.0
    return cm


def _chunks(w):
    n = w.shape[1] // 512
    return np.ascontiguousarray(w.reshape(8, 128, n, 512).transpose(2, 1, 0, 3))


def _fm(v):
    return np.ascontiguousarray(np.asarray(v, np.float32).reshape(8, 128).T)


def make_in_maps(inputs, ntok=SEQ, ncores=8):
    f = lambda a: np.asarray(a, np.float32)
    wbig = np.concatenate([_chunks(f(inputs["w_in"])[0]), _chunks(f(inputs["w_hg_branch"])[0]), _chunks(f(inputs["w_out"])[0]),
                           _chunks(f(inputs["w_query"])[0]), _chunks(f(inputs["w_ple_gate"])[0])], axis=0)
    wpool = np.ascontiguousarray(f(inputs["pool_w"])[0].transpose(1, 0, 2))
    wpp = np.ascontiguousarray(f(inputs["w_ple_proj"])[0].reshape(2, 128, 1024).transpose(1, 0, 2))
    sk = np.ascontiguousarray(f(inputs["sub_keys"])[0].reshape(16, 128, 128).transpose(1, 0, 2))
    cvec = np.zeros((128, 80), np.float32)
    cvec[:, 0:8] = _fm(inputs["ln0_g"]); cvec[:, 8:16] = _fm(inputs["ln0_b"])
    cvec[:, 16:24] = _fm(f(inputs["ln1_g"])[0]); cvec[:, 24:32] = _fm(f(inputs["ln1_b"])[0])
    cvec[:, 32:40] = _fm(f(inputs["pool_scale"])[0])
    cvec[:, 40:48] = _fm(f(inputs["hg_lb"])[0]); cvec[:, 48:56] = _fm(f(inputs["hg_lb"])[1])
    cvec[:, 56] = f(inputs["hg_norm_g"])[0]
    bc2 = np.ascontiguousarray(np.broadcast_to(np.concatenate([f(inputs["ln2_g"])[0], f(inputs["ln2_b"])[0]])[None, :], (128, 2048)))
    cm = _consts()
    x = f(inputs["x"]); p = f(inputs["p"])[0]
    u = f(inputs["u_tab"])[0]; v = f(inputs["v_tab"])[0]
    maps = []
    for c in range(ncores):
        maps.append(dict(x=np.ascontiguousarray(x[c, :ntok]), p=np.ascontiguousarray(p[c, :ntok]), wbig=wbig, wpool=wpool, wpp=wpp, sk=sk,
                         u_tab=u, v_tab=v, cvec=cvec, cmat=cm, bc2=bc2))
    return maps


def kernel(**inputs):
    nc = build_program(SEQ)
    maps = make_in_maps(inputs, SEQ, 8)
    res = run_bass_kernel_spmd(nc, maps, core_ids=list(range(8)))
    return np.stack([np.asarray(r["y"], np.float32) for r in res.results], axis=0)
```

```python
from contextlib import ExitStack
import numpy as np
import concourse.bass as bass
import concourse.mybir as mybir
from concourse.bass_utils import run_bass_kernel_spmd

F32 = mybir.dt.float32
BF16 = mybir.dt.bfloat16
U32 = mybir.dt.uint32
I32 = mybir.dt.int32
ALU = mybir.AluOpType
AF = mybir.ActivationFunctionType
AX = mybir.AxisListType

SEQ = 4096
D = 1024
ALPHA = 2.0 ** 0.25
NCHUNK_W = 23
SB_BASE = 16512
SB_END = 229376
STRICT_SAME_ENGINE = True


class Sched:
    def __init__(self):
        self.ops = []
        self.last_w = {}
        self.readers = {}

    def add(self, eng, fn, r=(), w=(), dsem=None):
        idx = len(self.ops)
        deps = set()
        for res in r:
            lw = self.last_w.get(res)
            if lw is not None:
                deps.add((lw, 'raw'))
        for res in w:
            lw = self.last_w.get(res)
            if lw is not None:
                deps.add((lw, 'waw'))
            for rd in self.readers.get(res, ()):
                deps.add((rd, 'war'))
        fdeps = set()
        for d, kind in deps:
            de = self.ops[d]['eng']
            if de == eng and eng != 'sp' and self.ops[d]['dsem'] is None and dsem is None:
                if eng == 'pe' or (kind != 'raw' and not STRICT_SAME_ENGINE):
                    continue
            fdeps.add(d)
        self.ops.append(dict(eng=eng, fn=fn, deps=fdeps, dsem=dsem))
        for res in r:
            self.readers.setdefault(res, []).append(idx)
        for res in w:
            self.last_w[res] = idx
            self.readers[res] = []
        return idx

    def emit(self, nc, stack):
        ops = self.ops
        needed = set()
        for op in ops:
            needed |= op['deps']
        sems = {}

        def getsem(key):
            if key not in sems:
                sems[key] = stack.enter_context(nc.semaphore("s%d" % len(sems)))
            return sems[key]

        cnt = {}
        tot = {}
        for op in ops:
            if op['dsem'] is not None:
                tot[op['dsem'][0]] = tot.get(op['dsem'][0], 0) + 1
        for i, op in enumerate(ops):
            if op['dsem'] is not None:
                key, mode = op['dsem']
                cnt[key] = cnt.get(key, 0) + 1
                op['sem'] = getsem(('d', key))
                op['tick'] = 16 * (cnt[key] if mode == 'slot' else tot[key])
                op['inc'] = 16
            elif i in needed:
                e = op['eng']
                cnt[e] = cnt.get(e, 0) + 1
                op['sem'] = getsem(('e', e))
                op['tick'] = cnt[e]
                op['inc'] = 1
            else:
                op['inc'] = 0
        self.nsems = len(sems)
        per = {e: [] for e in ('pe', 'act', 'dve', 'pool', 'sp')}
        for op in ops:
            per[op['eng']].append(op)
        block = stack.enter_context(nc.Block())

        def run(eh, lst):
            waited = {}
            for op in lst:
                w = {}
                for d in op['deps']:
                    dop = ops[d]
                    s = dop['sem']
                    if dop['tick'] > w.get(s, 0):
                        w[s] = dop['tick']
                for s, t in w.items():
                    if t > waited.get(s, 0):
                        eh.wait_ge(s, t)
                        waited[s] = t
                ins = op['fn'](eh)
                if op['inc']:
                    ins.then_inc(op['sem'], op['inc'])

        @block.tensor
        def _(e):
            run(e, per['pe'])

        @block.scalar
        def _(e):
            run(e, per['act'])

        @block.vector
        def _(e):
            run(e, per['dve'])

        @block.gpsimd
        def _(e):
            run(e, per['pool'])

        @block.sync
        def _(e):
            run(e, per['sp'])


class Arena:
    def __init__(self, nc, base, end):
        self.nc, self.base, self.end, self.n = nc, base, end, 0

    def at(self, off, name, shape, dt):
        self.n += 1
        return self.nc.alloc_sbuf_tensor_at("%s_%d" % (name, self.n), shape, dt, offset=off)


def _nbytes(shape, dt):
    n = 1
    for s in shape[1:]:
        n *= s
    return n * (2 if dt == BF16 else 4)


class Bump:
    def __init__(self, arena, start, end):
        self.a, self.p, self.end = arena, start, end
        self.off = {}

    def __call__(self, name, shape, dt):
        nb = (_nbytes(shape, dt) + 63) // 64 * 64
        off = self.p
        self.off[name] = off
        self.p += nb
        assert self.p <= self.end, ("SBUF overflow", name, self.p, self.end)
        return self.a.at(off, name, shape, dt)


def build_program(NTOK=SEQ, debug=False):
    NT = NTOK // 256
    nc = bass.Bass("TRN2", target_bir_lowering=False)
    dr = lambda name, shape, dt, kind: nc.dram_tensor(name, shape, dt, kind=kind).ap()
    x_d = dr("x", [NTOK, D], F32, "ExternalInput")
    p_d = dr("p", [NTOK, 256], F32, "ExternalInput")
    wbig_d = dr("wbig", [NCHUNK_W, 128, 8, 512], F32, "ExternalInput")
    wpool_d = dr("wpool", [128, 4, 256], F32, "ExternalInput")
    wpp_d = dr("wpp", [128, 2, 1024], F32, "ExternalInput")
    sk_d = dr("sk", [128, 16, 128], F32, "ExternalInput")
    u_d = dr("u_tab", [16384, D], F32, "ExternalInput")
    v_d = dr("v_tab", [16384, D], F32, "ExternalInput")
    cvec_d = dr("cvec", [128, 80], F32, "ExternalInput")
    cmat_d = dr("cmat", [128, 2324], F32, "ExternalInput")
    bc2_d = dr("bc2", [128, 2048], F32, "ExternalInput")
    y_d = dr("y", [NTOK, D], F32, "ExternalOutput")
    kd = "ExternalOutput" if debug else "Internal"
    wscr_d = dr("wscr", [NCHUNK_W, 128, 8, 512], BF16, "Internal")
    uT_d = dr("uT_s", [128, 128, 8, 128], BF16, "Internal")
    vs_d = dr("v_s", [128, 128, 1024], BF16, "Internal")
    h1T_d = dr("h1T_s", [128, 8, NTOK], BF16, kd)
    z0_d = dr("z0_s", [NTOK, D], F32, kd)

    S = Sched()
    st = ExitStack()
    A = Arena(nc, SB_BASE, SB_END)
    ps = nc.alloc_psum_tensor("ps", [128, 4096], F32)
    psb = ps[:, :].bitcast(BF16)

    def bank(b):
        return ps[:, b * 512:(b + 1) * 512]

    def bankb(b):
        return psb[:, b * 1024:(b + 1) * 1024]

    bctr = [0]

    def nb(n=1):
        if n == 2 and bctr[0] % 2 == 1:
            bctr[0] += 1
        b = bctr[0] % 8
        bctr[0] += n
        return b

    def PB(b):
        return 'pb%d' % b

    add = S.add

    def dma(out, in_, r, w, key, mode='slot'):
        add('sp', lambda e: e.dma_start(out=out, in_=in_), r=r, w=w, dsem=(key, mode))

    def mm(out, lhsT, rhs, start, stop, r, w):
        add('pe', lambda e: e.matmul(out, lhsT=lhsT, rhs=rhs, start=start, stop=stop), r=r, w=w)

    def tr(out, in_, ident, r, w):
        add('pe', lambda e: e.transpose(out=out, in_=in_, identity=ident), r=r, w=w)

    def act(out, in_, func, r, w, scale=1.0, bias=0.0):
        add('act', lambda e: e.activation(out=out, in_=in_, func=func, scale=scale, bias=bias), r=r, w=w)

    def cp(eng, out, in_, r, w):
        if eng == 'act':
            act(out, in_, AF.Copy, r, w)
        else:
            add(eng, lambda e: e.tensor_copy(out=out, in_=in_), r=r, w=w)

    def tt(eng, out, a, b, op, r, w):
        add(eng, lambda e: e.tensor_tensor(out=out, in0=a, in1=b, op=op), r=r, w=w)

    def ts(eng, out, a, s1, s2, op0, op1, r, w):
        add(eng, lambda e: e.tensor_scalar(out=out, in0=a, scalar1=s1, scalar2=s2, op0=op0, op1=op1), r=r, w=w)

    def stt(out, a, s, b, op0, op1, r, w):
        add('dve', lambda e: e.scalar_tensor_tensor(out=out, in0=a, scalar=s, in1=b, op0=op0, op1=op1), r=r, w=w)

    pers = Bump(A, SB_BASE, SB_END)
    cvec = pers("cvec", [128, 80], F32)
    identf = pers("identf", [128, 128], F32)
    identb = pers("identb", [128, 128], BF16)
    iotab = pers("iotab", [128, 128], BF16)
    iota16 = pers("iota16", [128, 16], F32)
    cmask4 = pers("cmask4", [128, 4], F32)
    I1T = pers("I1T", [128, NTOK], BF16)
    I2T = pers("I2T", [128, NTOK], BF16)
    GT = pers("GT", [128, NTOK], BF16)
    P_END = pers.p
    C_G0, C_B0, C_G1, C_B1, C_PSC, C_LB0, C_LB1, C_NG, C_OML, C_NOML = 0, 8, 16, 24, 32, 40, 48, 56, 57, 65
    M_ID, M_MASK, M_RESET, M_BANDS, M_I16, M_I128, M_CM4, M_ONES = 0, 128, 256, 512, 2048, 2064, 2192, 2196

    p0 = Bump(A, P_END, SB_END)
    cmat = p0("cmat", [128, 2324], F32)
    dma(cvec[:, :], cvec_d, [], ['cvec'], 'cvec')
    dma(cmat[:, :], cmat_d, [], ['cmat0'], 'cmat0')
    cp('dve', identf[:, :], cmat[:, M_ID:M_ID + 128], ['cmat0'], ['identf'])
    cp('dve', identb[:, :], cmat[:, M_ID:M_ID + 128], ['cmat0'], ['identb'])
    cp('dve', iotab[:, :], cmat[:, M_I128:M_I128 + 128], ['cmat0'], ['iotab'])
    cp('dve', iota16[:, :], cmat[:, M_I16:M_I16 + 16], ['cmat0'], ['iota16'])
    cp('dve', cmask4[:, :], cmat[:, M_CM4:M_CM4 + 4], ['cmat0'], ['cmask4'])
    tt('dve', cvec[:, C_OML:C_OML + 8], cvec[:, C_LB1:C_LB1 + 8], cvec[:, C_LB0:C_LB0 + 8], ALU.subtract, ['cvec'], ['cvec'])
    act(cvec[:, C_OML:C_OML + 8], cvec[:, C_OML:C_OML + 8], AF.Sigmoid, ['cvec'], ['cvec'])
    ts('dve', cvec[:, C_NOML:C_NOML + 8], cvec[:, C_OML:C_OML + 8], -1.0, None, ALU.mult, ALU.bypass, ['cvec'], ['cvec'])

    for c in (0, 1, 6, 7, 2, 3, 9, 10, 11, 12, 4, 5, 8, 13, 14, 15, 16, 21, 22, 17, 18, 19, 20):
        add('pool', lambda e, c=c: e.dma_start(out=wscr_d[c], in_=wbig_d[c]), w=['wscr%d' % c], dsem=('wc%d' % c, 'slot'))

    pa = Bump(A, P_END, SB_END)
    P0RES = ['cmat0']
    maskTb = pa("maskTb", [128, 128], BF16)
    resetm = pa("resetm", [128, 256], F32)
    bands = pa("bands", [128, 12, 128], BF16)
    onesb = pa("onesb", [128, 128], BF16)
    pwb = pa("pwb", [128, 4, 256], BF16)
    wppb = pa("wppb", [128, 2, 1024], BF16)
    skT = pa("skT", [128, 16, 128], BF16)
    wslot = [pa("wslot%d" % i, [128, 8, 512], BF16) for i in range(2)]
    xt = [pa("xt0", [128, 1024], F32)]
    eqb = pa("eqb", [128, 8, 16, 16], BF16)
    lnst = pa("lnst", [128, 2, 6], F32)
    lnmv = pa("lnmv", [128, 2], F32)
    lnr = pa("lnr", [128, 1], F32)
    hT32 = pa("hT32", [128, 8, 256], F32)
    hTb = pa("hTb", [128, 8, 256], BF16)
    qsT = pa("qsT", [128, 8, 256], BF16)
    sc = pa("sc", [128, 16, 128], F32)
    vtm = pa("vtm", [128, 2, 1024], BF16)
    sgT = pa("sgT", [128, 8, 256], BF16)
    vp = pa("vp", [128, 3, 512], BF16)
    gaT = pa("gaT", [128, 8, 256], BF16)
    gbT = pa("gbT", [128, 8, 256], BF16)
    E1 = pa("E1", [128, 8, 256], F32)
    E2 = pa("E2", [128, 8, 256], F32)
    bhT_off = pa.p
    bhT = pa("bhT", [128, 8, 256], F32)
    qpT = A.at(bhT_off, "qpT", [128, 16, 256], BF16)
    qdT = pa("qdT", [128, 8, 256], BF16)
    kiT = pa("kiT", [128, 8, 256], BF16)
    kendT = pa("kendT", [128, 8, 256], BF16)
    kendm = pa("kendm", [128, 4, 1024], BF16)
    scm = pa("scm", [128, 8, 128], BF16)
    S32 = pa("S32", [128, 8, 128], F32)
    Sbs = pa("Sbs", [128, 5, 8, 128], BF16)
    dec = pa("dec", [128, 8, 8], F32)
    sqb = pa("sqb", [128, 8, 128], BF16)
    rstd = pa("rstd", [128, 8, 128], F32)
    on = pa("on", [128, 8, 128], F32)
    onT = pa("onT", [128, 8, 256], BF16)
    plT = pa("plT", [128, 4, 256], BF16)
    mixT = A.at(pa.off["kiT"], "mixT", [128, 8, 256], BF16)
    zb = A.at(pa.off["kendT"], "zb", [128, 8, 256], BF16)
    lnrs = pa("lnrs", [128, 256], F32)
    pld = pa("pld", [128, 2, 256], F32)
    pldb = pa("pldb", [128, 2, 256], BF16)
    pT = pa("pT", [128, 2, 256], BF16)
    z0tm = A.at(pa.off["kendm"], "z0tm", [128, 1024], F32)
    tkv = pa("tkv", [128, 8, 2, 16], F32)
    tki = pa("tki", [128, 8, 2, 16], U32)
    tkif = pa("tkif", [128, 8, 2, 16], F32)
    tmp128 = pa("tmp128", [128, 128], F32)
    cand = pa("cand", [128, 16, 16], F32)
    cand2 = pa("cand2", [128, 16, 16], F32)
    bv = pa("bv", [128, 8, 16], F32)
    posu = pa("posu", [128, 8, 16], U32)
    posr = pa("posr", [128, 8, 16], U32)
    posc = pa("posc", [128, 8, 16], U32)
    rf = pa("rf", [128, 8, 16], F32)
    cf = pa("cf", [128, 8, 16], F32)
    i12g = pa("i12g", [128, 3, 128], F32)
    i12gb = pa("i12gb", [128, 3, 128], BF16)
    zsum = pa("zsum", [128, 8], F32)
    tmp128b = A.at(pa.off["tkif"], "tmp128b", [128, 128], F32)
    candb = A.at(pa.off["rf"], "candb", [128, 16, 16], F32)
    cand2b = A.at(pa.off["posr"], "cand2b", [128, 16, 16], F32)
    assert pa.off["cf"] == pa.off["rf"] + 512 and pa.off["posc"] == pa.off["posr"] + 512
    bvs = pa("bvs", [128, 8, 16], F32)
    ub2 = pa("ub2", [128, 2, 1024], BF16)
    uT2 = pa("uT2", [128, 2, 8, 128], BF16)
    cmatA = A.at(pa.off["wslot0"], "cmatA", [128, 2324], F32)
    skst = A.at(pa.off["E1"], "skst", [128, 16, 128], F32)
    skb = A.at(pa.off["qdT"], "skb", [128, 16, 128], BF16)
    wst = A.at(pa.off["E2"], "wst", [128, 2048], F32)

    CM = ['ws0', 'ws1']
    dma(cmatA[:, :], cmat_d, P0RES, CM + P0RES, 'cmatA')
    cp('dve', maskTb[:, :], cmatA[:, M_MASK:M_MASK + 128], CM, ['maskTb'])
    cp('dve', resetm[:, :], cmatA[:, M_RESET:M_RESET + 256], CM, ['resetm'])
    cp('dve', bands[:, :, :], cmatA[:, M_BANDS:M_BANDS + 1536].rearrange("p (a b) -> p a b", a=12), CM, ['bands'])
    cp('dve', onesb[:, :], cmatA[:, M_ONES:M_ONES + 128], CM, ['onesb'])
    dma(wst[:, 0:1024], wpool_d.rearrange("p a b -> p (a b)"), P0RES, ['E2'], 'wst')
    cp('dve', pwb[:, :, :], wst[:, 0:1024].rearrange("p (a b) -> p a b", a=4), ['E2'], ['pwb'])
    dma(wst[:, :], wpp_d.rearrange("p a b -> p (a b)"), ['E2'], ['E2'], 'wst')
    cp('dve', wppb[:, :, :], wst[:, :].rearrange("p (a b) -> p a b", a=2), ['E2'], ['wppb'])
    dma(skst[:, :, :], sk_d, P0RES, ['E1'], 'skst')
    cp('dve', skb[:, :, :], skst[:, :, :], ['E1'], ['qdT'])
    for q4 in range(4):
        bk = nb()
        for j in range(4):
            hp = q4 * 4 + j
            tr(bankb(bk)[:, j * 128:(j + 1) * 128], skb[:, hp, :], identb[:, :], ['qdT', 'identb'], [PB(bk)])
        cp('act', skT[:, q4 * 4:(q4 + 1) * 4, :], bankb(bk)[:, 0:512].rearrange("p (a b) -> p a b", a=4), [PB(bk)], ['skT'])
    add('dve', lambda e: e.memset(S32[:, :, :], 0.0), w=['S32'])
    add('pool', lambda e: e.memset(Sbs[:, 0, :, :], 0.0), w=['Sbs0'])
    add('pool', lambda e: e.memset(vp[:, 0, :], 0.0), w=['vp'])


    ust = dict(loaded=-1, done=-1)

    def u_step():
        if ust['done'] < ust['loaded']:
            g = ust['loaded']
            for b in range(2):
                bk = nb()
                for k in range(8):
                    tr(bankb(bk)[:, k * 128:(k + 1) * 128], ub2[:, b, k * 128:(k + 1) * 128], identb[:, :], ['ub2', 'identb'], [PB(bk)])
                cp('act', uT2[:, b, :, :], bankb(bk).rearrange("p (k e) -> p k e", k=8), [PB(bk)], ['uT2'])
            dma(uT_d[g * 2:(g + 1) * 2].rearrange("b p k e -> p b k e"), uT2[:, :, :, :], ['uT2'], ['uT_s'], 'uT_s')
            ust['done'] = g
        if ust['loaded'] < 63:
            g = ust['loaded'] + 1
            add('pool', lambda e, g=g: e.dma_start(out=ub2[:, :, :], in_=u_d[g * 256:(g + 1) * 256, :].rearrange("(b p) d -> p b d", p=128)),
                w=['ub2'], dsem=('ub2', 'slot'))
            ust['loaded'] = g

    vst = [0]

    def v_step(n=2):
        for _ in range(n):
            if vst[0] < 32:
                g = vst[0]
                add('pool', lambda e, g=g: e.dma_start(out=vs_d[g * 4:(g + 1) * 4].rearrange("b e d -> (b e) d"), in_=v_d[g * 512:(g + 1) * 512, :]),
                    w=['v_s'], dsem=('vcast', 'slot'))
                vst[0] += 1

    wuse = [0]

    def wload(c):
        s = wuse[0] % 2
        wuse[0] += 1
        dma(wslot[s][:, :, :], wscr_d[c], ['wscr%d' % c], ['ws%d' % s], 'ws%d' % s)
        return wslot[s], 'ws%d' % s

    def proj_fm(c, rhsT, rres, evac):
        wt, wr = wload(c)
        for half in range(2):
            bk = nb()
            for j in range(2):
                m = half * 2 + j
                for k in range(8):
                    mm(bank(bk)[:, j * 256:(j + 1) * 256], wt[:, k, m * 128:(m + 1) * 128], rhsT[:, k, :], k == 0, k == 7, [wr] + rres, [PB(bk)])
            evac(bk, half)

    def b3(bk):
        return bank(bk).rearrange("p (a b) -> p a b", a=2)

    RMS_EPS = 1e-6
    LN_EPS = 1e-5
    def gen_X1(t):
        T0 = t * 256
        u_step()
        v_step(2)
        for sub in range(2):
            xs = xt[0]
            xr = 'xt0'
            r0 = T0 + sub * 128
            dma(xs[:, :], x_d[r0:r0 + 128, :], [], [xr], xr)
            add('dve', lambda e, xs=xs: e.bn_stats(out=lnst[:, 0, :], in_=xs[:, 0:512]), r=[xr], w=['lnst'])
            add('dve', lambda e, xs=xs: e.bn_stats(out=lnst[:, 1, :], in_=xs[:, 512:1024]), r=[xr], w=['lnst1'])
            add('dve', lambda e: e.bn_aggr(out=lnmv[:, :], in_=lnst[:, :, :].rearrange("p a b -> p (a b)")), r=['lnst', 'lnst1'], w=['lnmv'])
            act(lnr[:, :], lnmv[:, 1:2], AF.Sqrt, ['lnmv'], ['lnr'], bias=LN_EPS)
            add('dve', lambda e: e.reciprocal(out=lnr[:, :], in_=lnr[:, :]), r=['lnr'], w=['lnr'])
            ts('dve', xs[:, :], xs[:, :], lnmv[:, 0:1], lnr[:, 0:1], ALU.subtract, ALU.mult, [xr, 'lnmv', 'lnr'], [xr])
            bk = nb(2)
            for k in range(8):
                tr(ps[:, bk * 512 + k * 128: bk * 512 + (k + 1) * 128], xs[:, k * 128:(k + 1) * 128], identf[:, :], [xr, 'identf'], [PB(bk), PB(bk + 1)])
            for k in range(8):
                act(hT32[:, k, sub * 128:(sub + 1) * 128], ps[:, bk * 512 + k * 128: bk * 512 + (k + 1) * 128], AF.Identity,
                    [PB(bk), PB(bk + 1), 'cvec'], ['hT32'], scale=cvec[:, C_G0 + k:C_G0 + k + 1], bias=cvec[:, C_B0 + k:C_B0 + k + 1])
            yield
        cp('act', hTb[:, :, :], hT32[:, :, :], ['hT32'], ['hTb'])
        yield

        def ev_act(dst, dres, func, scale=1.0):
            def f(bk, half, c0):
                act(dst[:, c0 + half * 2:c0 + half * 2 + 2, :], b3(bk), func, [PB(bk)], [dres], scale=scale)
            return f
        for c in (0, 1):
            proj_fm(c, hTb, ['hTb'], lambda bk, half, c=c: ev_act(qsT, 'qsT', AF.Silu)(bk, half, c * 4))
            yield
        for c in (6, 7):
            proj_fm(c, hTb, ['hTb'], lambda bk, half, c=c: ev_act(sgT, 'sgT', AF.Silu)(bk, half, (c - 6) * 4))
            yield
        for c in (2, 3):
            proj_fm(c, hTb, ['hTb'], lambda bk, half, c=c: ev_act(E2, 'E2', AF.Sigmoid, -1.0)(bk, half, (c - 2) * 4))
            yield
        yield from gen_A3(t)
        for c in (9, 10):
            proj_fm(c, hTb, ['hTb'], lambda bk, half, c=c: ev_act(gaT, 'gaT', AF.Sigmoid)(bk, half, (c - 9) * 4))
            yield
        for c in (11, 12):
            proj_fm(c, hTb, ['hTb'], lambda bk, half, c=c: ev_act(gbT, 'gbT', AF.Sigmoid)(bk, half, (c - 11) * 4))
            yield
        for c in (4, 5, 8):
            wt, wr = wload(c)
            for sub in range(2):
                bk = nb()
                for k in range(8):
                    mm(bank(bk), hTb[:, k, sub * 128:(sub + 1) * 128], wt[:, k, :], k == 0, k == 7, [wr, 'hTb'], [PB(bk)])
                if c == 8:
                    cp('act', vp[:, 1 + sub, :], bank(bk), [PB(bk)], ['vp'])
                else:
                    cp('act', vtm[:, sub, (c - 4) * 512:(c - 3) * 512], bank(bk), [PB(bk)], ['vtm'])
            yield

    def gen_A3(t):
        T0 = t * 256
        yield 'WAIT_SC1'
        u_step()
        for h in range(8):
            act(E1[:, h, :], E2[:, h, :], AF.Ln, ['E2', 'cvec'], ['E1'], scale=cvec[:, C_NOML + h:C_NOML + h + 1], bias=1.0)
        tt('pool', kiT[:, :, :], E2[:, :, :], cvec[:, C_OML:C_OML + 8].unsqueeze(2).to_broadcast([128, 8, 256]), ALU.mult, ['E2', 'cvec'], ['kiT'])
        for h in range(8):
            add('dve', lambda e, h=h: e.tensor_tensor_scan(out=bhT[:, h, :], data0=resetm[:, :], data1=E1[:, h, :], initial=0.0,
                                                             op0=ALU.mult, op1=ALU.add), r=['E1', 'resetm'], w=['bhT'])
        yield
        bh3 = bhT[:, :, :].rearrange("p h (c s) -> p (h c) s", s=32)
        tt('dve', E2[:, :, :].rearrange("p h (c s) -> p (h c) s", s=32), bh3[:, :, 31:32].to_broadcast([128, 64, 32]), bh3, ALU.subtract,
           ['bhT', 'kiT'], ['E2'])
        act(E2[:, :, :], E2[:, :, :], AF.Exp, ['E2'], ['E2'])
        tt('dve', kendT[:, :, :], kiT[:, :, :], E2[:, :, :], ALU.mult, ['kiT', 'E2'], ['kendT'])
        act(E1[:, :, :], bhT[:, :, :], AF.Exp, ['bhT'], ['E1'])
        stt(qdT[:, :, :], qsT[:, :, :], 128.0 ** -0.5, E1[:, :, :], ALU.mult, ALU.mult, ['qsT', 'E1'], ['qdT'])
        yield
        act(E2[:, :, :], bhT[:, :, :], AF.Exp, ['bhT', 'kendT'], ['E2'], scale=-1.0)
        tt('pool', kiT[:, :, :], kiT[:, :, :], E2[:, :, :], ALU.mult, ['kiT', 'E2'], ['kiT'])
        act(dec[:, :, :].rearrange("p h c -> p (h c)"), bh3[:, :, 31], AF.Exp, ['bhT'], ['dec'])
        yield

    def gen_Sc(t):
        T0 = t * 256
        for sub in range(2):
            gsub = t * 2 + sub
            par = gsub % 2
            bk = nb()
            for h in range(8):
                tr(bankb(bk)[:, h * 128:(h + 1) * 128], kendT[:, h, sub * 128:(sub + 1) * 128], identb[:, :], ['kendT', 'identb'], [PB(bk)])
            for cc in range(4):
                act(kendm[:, cc, :], bankb(bk), AF.Copy, [PB(bk), 'cmask4'], ['kendm'], scale=cmask4[:, cc:cc + 1])
            for cc in range(4):
                db = nb(2)
                for h in range(8):
                    mm(ps[:, db * 512 + h * 128: db * 512 + (h + 1) * 128], kendm[:, cc, h * 128:(h + 1) * 128], vtm[:, sub, h * 128:(h + 1) * 128],
                       True, True, ['kendm', 'vtm'], [PB(db), PB(db + 1)])
                gc = sub * 4 + cc
                tt('dve', S32[:, :, :], S32[:, :, :], dec[:, :, gc:gc + 1].to_broadcast([128, 8, 128]), ALU.mult, ['S32', 'dec'], ['S32'])
                tt('dve', S32[:, :, :], S32[:, :, :], ps[:, db * 512:(db + 2) * 512].rearrange("p (h e) -> p h e", h=8), ALU.add,
                   ['S32', PB(db), PB(db + 1)], ['S32'])
                nslot = cc + 1 if cc < 3 else (4 if par == 0 else 0)
                cp('act', Sbs[:, nslot, :, :], S32[:, :, :], ['S32'], ['Sbs%d' % nslot])
            yield
            sb_ = [nb(), nb()]
            for h in range(8):
                mm(bank(sb_[h // 4])[:, (h % 4) * 128:(h % 4 + 1) * 128], kiT[:, h, sub * 128:(sub + 1) * 128], qdT[:, h, sub * 128:(sub + 1) * 128],
                   True, True, ['kiT', 'qdT'], [PB(sb_[h // 4])])
            for g2 in range(2):
                tt('dve', scm[:, g2 * 4:(g2 + 1) * 4, :], bank(sb_[g2]).rearrange("p (a b) -> p a b", a=4),
                   maskTb[:, :].unsqueeze(1).to_broadcast([128, 4, 128]), ALU.mult, [PB(sb_[g2]), 'maskTb'], ['scm'])
            ob = [nb(), nb()]
            for h in range(8):
                reg = bank(ob[h // 4])[:, (h % 4) * 128:(h % 4 + 1) * 128]
                mm(reg, vtm[:, sub, h * 128:(h + 1) * 128], scm[:, h, :], True, False, ['vtm', 'scm'], [PB(ob[h // 4])])
                for cc in range(4):
                    slot = 4 if (par == 1 and cc == 0) else cc
                    mm(reg[:, cc * 32:(cc + 1) * 32], Sbs[:, slot, h, :], qdT[:, h, sub * 128 + cc * 32: sub * 128 + (cc + 1) * 32], False, cc == 3,
                       ['Sbs%d' % slot, 'qdT'], [PB(ob[h // 4])])
            for g2 in range(2):
                act(sqb[:, g2 * 4:(g2 + 1) * 4, :], bank(ob[g2]).rearrange("p (a b) -> p a b", a=4), AF.Square, [PB(ob[g2])], ['sqb'])
                mb = nb()
                mm(bank(mb), onesb[:, :], sqb[:, g2 * 4:(g2 + 1) * 4, :].rearrange("p a b -> p (a b)"), True, True, ['onesb', 'sqb'], [PB(mb)])
                act(rstd[:, g2 * 4:(g2 + 1) * 4, :], bank(mb).rearrange("p (a b) -> p a b", a=4), AF.Sqrt, [PB(mb)], ['rstd'], bias=RMS_EPS)
            add('dve', lambda e: e.reciprocal(out=rstd[:, :, :], in_=rstd[:, :, :]), r=['rstd'], w=['rstd'])
            for g2 in range(2):
                tt('dve', on[:, g2 * 4:(g2 + 1) * 4, :], bank(ob[g2]).rearrange("p (a b) -> p a b", a=4), rstd[:, g2 * 4:(g2 + 1) * 4, :], ALU.mult,
                   [PB(ob[g2]), 'rstd'], ['on'])
            stt(onT[:, :, sub * 128:(sub + 1) * 128], on[:, :, :], cvec[:, C_NG:C_NG + 1], sgT[:, :, sub * 128:(sub + 1) * 128], ALU.mult, ALU.mult,
                ['on', 'cvec', 'sgT'], ['onT'])
            yield

        u_step()
        for g2 in range(2):
            bk = nb()
            for gg in range(2):
                g = g2 * 2 + gg
                for sub in range(2):
                    first = (t == 0 and sub == 0)
                    reg = bank(bk)[:, gg * 256 + sub * 128: gg * 256 + (sub + 1) * 128]
                    mm(reg, vp[:, 1 + sub, g * 128:(g + 1) * 128], bands[:, (8 + g) if first else g, :], True, first, ['vp', 'bands'], [PB(bk)])
                    if not first:
                        mm(reg, vp[:, sub, g * 128:(g + 1) * 128], bands[:, 4 + g, :], False, True, ['vp', 'bands'], [PB(bk)])
            cp('act', plT[:, g2 * 2:g2 * 2 + 2, :], b3(bk), [PB(bk)], ['plT'])
        cp('pool', vp[:, 0, :], vp[:, 2, :], ['vp', 'plT'], ['vp'])
        yield
        wa = [None, None]
        for q in range(4):
            if q % 2 == 0:
                wa = wload(13 + q // 2)
            wt, wr = wa
            ya = nb()
            for j in range(2):
                m = (q % 2) * 2 + j
                for k in range(8):
                    mm(bank(ya)[:, j * 256:(j + 1) * 256], wt[:, k, m * 128:(m + 1) * 128], onT[:, k, :], k == 0, k == 7, [wr, 'onT'], [PB(ya)])
            yb = nb()
            for j in range(2):
                mm(bank(yb)[:, j * 256:(j + 1) * 256], pwb[:, q, j * 128:(j + 1) * 128], plT[:, q, :], True, True, ['pwb', 'plT'], [PB(yb)])
            tt('dve', E1[:, 2 * q:2 * q + 2, :], b3(ya), gaT[:, 2 * q:2 * q + 2, :], ALU.mult, [PB(ya), 'gaT'], ['E1'])
            for j in range(2):
                m = 2 * q + j
                stt(E2[:, m, :], bank(yb)[:, j * 256:(j + 1) * 256], cvec[:, C_PSC + m:C_PSC + m + 1], gbT[:, m, :], ALU.mult, ALU.mult,
                    [PB(yb), 'cvec', 'gbT'], ['E2'])
            yield
        tt('pool', mixT[:, :, :], E1[:, :, :], E2[:, :, :], ALU.add, ['E1', 'E2'], ['kiT'])

        for c in (15, 16):
            def evz(bk, half, c=c):
                m0 = (c - 15) * 4 + half * 2
                stt(hT32[:, m0:m0 + 2, :], hT32[:, m0:m0 + 2, :], ALPHA, b3(bk), ALU.mult, ALU.add, ['hT32', PB(bk)], ['hT32'])
            proj_fm(c, mixT, ['kiT'], evz)
            yield
        cp('act', zb[:, :, :], hT32[:, :, :], ['hT32'], ['kendT'])
        mb = nb()
        for k in range(8):
            mm(bank(mb)[:, 0:256], onesb[:, :], zb[:, k, :], k == 0, k == 7, ['onesb', 'kendT'], [PB(mb)])
        stt(hT32[:, :, :], bank(mb)[:, 0:256].unsqueeze(1).to_broadcast([128, 8, 256]), -0.125, hT32[:, :, :], ALU.mult, ALU.add,
            [PB(mb), 'hT32'], ['hT32'])
        act(zb[:, :, :], hT32[:, :, :], AF.Square, ['hT32'], ['kendT'])
        vb_ = nb()
        for k in range(8):
            mm(bank(vb_)[:, 0:256], onesb[:, :], zb[:, k, :], k == 0, k == 7, ['onesb', 'kendT'], [PB(vb_)])
        act(lnrs[:, :], bank(vb_)[:, 0:256], AF.Sqrt, [PB(vb_)], ['lnrs'], scale=0.125, bias=LN_EPS)
        add('dve', lambda e: e.reciprocal(out=lnrs[:, :], in_=lnrs[:, :]), r=['lnrs'], w=['lnrs'])
        tt('pool', hT32[:, :, :], hT32[:, :, :], lnrs[:, :].unsqueeze(1).to_broadcast([128, 8, 256]), ALU.mult, ['hT32', 'lnrs'], ['hT32'])
        for k in range(8):
            act(hT32[:, k, :], hT32[:, k, :], AF.Identity, ['hT32', 'cvec'], ['hT32'],
                scale=cvec[:, C_G1 + k:C_G1 + k + 1], bias=cvec[:, C_B1 + k:C_B1 + k + 1])
        cp('act', hTb[:, :, :], hT32[:, :, :], ['hT32'], ['hTb'])
        dma(h1T_d[:, :, T0:T0 + 256], hTb[:, :, :], ['hTb'], ['h1T_s'], 'h1T_s')
        yield

    def gen_A7(t):
        T0 = t * 256
        u_step()
        dma(pld[:, :, :], p_d[T0:T0 + 256, :].rearrange("(s p) c -> p s c", p=128), [], ['pld'], 'pld')
        cp('pool', pldb[:, :, :], pld[:, :, :], ['pld'], ['pldb'])
        bk = nb()
        for kk in range(2):
            for sub in range(2):
                tr(bankb(bk)[:, (kk * 2 + sub) * 128:(kk * 2 + sub + 1) * 128], pldb[:, sub, kk * 128:(kk + 1) * 128], identb[:, :],
                   ['pldb', 'identb'], [PB(bk)])
        cp('act', pT[:, :, :], bankb(bk)[:, 0:512].rearrange("p (k t) -> p k t", k=2), [PB(bk)], ['pT'])
        yield
        for q in range(4):
            bk = nb()
            for j in range(2):
                m = 2 * q + j
                for kk in range(2):
                    mm(bank(bk)[:, j * 256:(j + 1) * 256], wppb[:, kk, m * 128:(m + 1) * 128], pT[:, kk, :], kk == 0, kk == 1, ['wppb', 'pT'], [PB(bk)])
            cp('act', E2[:, 2 * q:2 * q + 2, :], b3(bk), [PB(bk), 'kiT'], ['E2'])
            yield
        for c in (21, 22):
            proj_fm(c, hTb, ['hTb'], lambda bk, half, c=c: act(E1[:, (c - 21) * 4 + half * 2:(c - 21) * 4 + half * 2 + 2, :], b3(bk), AF.Sigmoid,
                                                                 [PB(bk), 'kiT'], ['E1']))
            yield
        tt('pool', E1[:, :, :], E1[:, :, :], E2[:, :, :], ALU.mult, ['E1', 'E2'], ['E1'])
        stt(E1[:, :, :], hT32[:, :, :], ALPHA, E1[:, :, :], ALU.mult, ALU.add, ['hT32', 'E1'], ['E1'])
        for sub in range(2):
            bk = nb(2)
            for k in range(8):
                tr(ps[:, bk * 512 + k * 128: bk * 512 + (k + 1) * 128], E1[:, k, sub * 128:(sub + 1) * 128], identf[:, :], ['E1', 'identf'],
                   [PB(bk), PB(bk + 1)])
            cp('act', z0tm[:, :], ps[:, bk * 512:(bk + 2) * 512], [PB(bk), PB(bk + 1)], ['kendm'])
            dma(z0_d[T0 + sub * 128:T0 + (sub + 1) * 128, :], z0tm[:, :], ['kendm'], ['z0_s'], 'z0_s')
            yield

    def gen_Sd(t):
        T0 = t * 256
        u_step()
        for c in (17, 18, 19, 20):
            proj_fm(c, hTb, ['hTb'], lambda bk, half, c=c: cp('act', qpT[:, (c - 17) * 4 + half * 2:(c - 17) * 4 + half * 2 + 2, :], b3(bk),
                                                                [PB(bk)], ['bhT']))
            yield
    def gen_chain(t):
        T0 = t * 256
        for sub in range(2):
            for q4 in range(4):
                bk = nb()
                for j in range(4):
                    hp = q4 * 4 + j
                    mm(bank(bk)[:, j * 128:(j + 1) * 128], qpT[:, hp, sub * 128:(sub + 1) * 128], skT[:, hp, :], True, True, ['bhT', 'skT'], [PB(bk)])
                cp('act', sc[:, q4 * 4:(q4 + 1) * 4, :], bank(bk).rearrange("p (a b) -> p a b", a=4), [PB(bk)], ['sc'])
            if sub == 1:
                yield 'SC1'
            def head_ops(h, tmpb, cnd, cnd2, r_tmp, r_c, r_c2):
                ops_ = []
                TK, TKI, BV, PU = 'tk%d' % h, 'tki%d' % h, 'bv%d' % h, 'posu%d' % h
                for p_ in range(2):
                    hp = 2 * h + p_
                    ops_.append(lambda h=h, p_=p_, hp=hp: add('dve', lambda e: e.max(out=tkv[:, h, p_, 0:8], in_=sc[:, hp, :]), r=['sc'], w=[TK]))
                    ops_.append(lambda h=h, p_=p_, hp=hp: add('dve', lambda e: e.match_replace(out=tmpb[:, :], in_to_replace=tkv[:, h, p_, 0:8],
                                                                                             in_values=sc[:, hp, :], imm_value=-1e30), r=['sc', TK], w=r_tmp))
                    ops_.append(lambda h=h, p_=p_: add('dve', lambda e: e.max(out=tkv[:, h, p_, 8:16], in_=tmpb[:, :]), r=r_tmp, w=[TK]))
                    ops_.append(lambda h=h, p_=p_, hp=hp: add('dve', lambda e: e.max_index(out=tki[:, h, p_, 0:8], in_max=tkv[:, h, p_, 0:8],
                                                                                         in_values=sc[:, hp, :]), r=['sc', TK], w=[TKI]))
                    ops_.append(lambda h=h, p_=p_, hp=hp: add('dve', lambda e: e.max_index(out=tki[:, h, p_, 8:16], in_max=tkv[:, h, p_, 8:16],
                                                                                         in_values=sc[:, hp, :]), r=['sc', TK], w=[TKI]))
                cflat = cnd[:, :, :].rearrange("p a b -> p (a b)")
                c2flat = cnd2[:, :, :].rearrange("p a b -> p (a b)")
                ops_.append(lambda h=h: tt('dve', cnd[:, :, :], tkv[:, h, 0, :].unsqueeze(2).to_broadcast([128, 16, 16]),
                                           tkv[:, h, 1, :].unsqueeze(1).to_broadcast([128, 16, 16]), ALU.add, [TK], r_c))
                ops_.append(lambda h=h: add('dve', lambda e: e.max(out=bv[:, h, 0:8], in_=cflat), r=r_c, w=[BV]))
                ops_.append(lambda h=h: add('dve', lambda e: e.match_replace(out=c2flat, in_to_replace=bv[:, h, 0:8], in_values=cflat, imm_value=-1e30),
                                            r=r_c + [BV], w=r_c2))
                ops_.append(lambda h=h: add('dve', lambda e: e.max(out=bv[:, h, 8:16], in_=c2flat), r=r_c2, w=[BV]))
                ops_.append(lambda h=h: add('dve', lambda e: e.max_index(out=posu[:, h, 0:8], in_max=bv[:, h, 0:8], in_values=cflat), r=r_c + [BV], w=[PU]))
                ops_.append(lambda h=h: add('dve', lambda e: e.max_index(out=posu[:, h, 8:16], in_max=bv[:, h, 8:16], in_values=cflat), r=r_c + [BV], w=[PU]))
                return ops_
            for hh in range(0, 8, 2):
                oa = head_ops(hh, tmp128, cand, cand2, ['tmp128'], ['cand'], ['cand2'])
                ob_ = head_ops(hh + 1, tmp128b, candb, cand2b, ['tkif'], ['rf', 'cf'], ['posr', 'posc'])
                for iz, (fa, fb) in enumerate(zip(oa, ob_)):
                    fa()
                    fb()
                    if iz % 4 == 3:
                        yield
            TKS = ['tk%d' % h for h in range(8)]
            TKIS = ['tki%d' % h for h in range(8)]
            BVS = ['bv%d' % h for h in range(8)]
            PUS = ['posu%d' % h for h in range(8)]
            ts('dve', posr[:, :, :], posu[:, :, :], 4, None, ALU.logical_shift_right, ALU.bypass, PUS, ['posr'])
            ts('dve', posc[:, :, :], posu[:, :, :], 15, None, ALU.bitwise_and, ALU.bypass, PUS, ['posc'])
            cp('dve', rf[:, :, :], posr[:, :, :], ['posr'], ['rf'])
            cp('dve', cf[:, :, :], posc[:, :, :], ['posc'], ['cf'])
            cp('dve', tkif[:, :, :, :], tki[:, :, :, :], TKIS, ['tkif'])
            yield
            EQ = eqb[:, :, :, :]
            io4 = iota16[:, :].unsqueeze(1).unsqueeze(1).to_broadcast([128, 8, 16, 16])
            for which, src in ((0, rf), (1, cf)):
                tt('dve', EQ, src[:, :, :].unsqueeze(3).to_broadcast([128, 8, 16, 16]), io4, ALU.is_equal, ['rf', 'cf', 'iota16'], ['eqb'])
                tt('dve', EQ, EQ, tkif[:, :, which, :].unsqueeze(2).to_broadcast([128, 8, 16, 16]), ALU.mult, ['eqb', 'tkif'], ['eqb'])
                add('dve', lambda e, which=which, EQ=EQ: e.tensor_reduce(out=i12g[:, which, :].rearrange("p (h k) -> p h k", h=8), in_=EQ,
                                                                          axis=AX.X, op=ALU.add), r=['eqb'], w=['i12g'])
            tt('dve', bvs[:, :, :], bv[:, :, :], bv[:, :, 0:1].to_broadcast([128, 8, 16]), ALU.subtract, BVS + PUS, ['bvs'])
            act(bvs[:, :, :], bvs[:, :, :], AF.Exp, ['bvs'], ['bvs'])
            add('dve', lambda e: e.tensor_reduce(out=zsum[:, :], in_=bvs[:, :, :], axis=AX.X, op=ALU.add), r=['bvs'], w=['zsum'])
            add('dve', lambda e: e.reciprocal(out=zsum[:, :], in_=zsum[:, :]), r=['zsum'], w=['zsum'])
            tt('dve', i12g[:, 2, :].rearrange("p (h k) -> p h k", h=8), bvs[:, :, :], zsum[:, :].unsqueeze(2).to_broadcast([128, 8, 16]), ALU.mult,
               ['bvs', 'zsum', 'i12g'], ['i12g'])
            yield
            cp('pool', i12gb[:, :, :], i12g[:, :, :], ['i12g'], ['i12gb'])
            bk = nb()
            for j in range(3):
                tr(bankb(bk)[:, j * 128:(j + 1) * 128], i12gb[:, j, :], identb[:, :], ['i12gb', 'identb'], [PB(bk)])
            tk0 = T0 + sub * 128
            for j, dst in enumerate((I1T, I2T, GT)):
                cp('act', dst[:, tk0:tk0 + 128], bankb(bk)[:, j * 128:(j + 1) * 128], [PB(bk)], ['idxT'])


    def drain(g):
        for _ in g:
            pass

    def seq(*gens):
        for g in gens:
            if g is not None:
                yield from g

    def merge(g1, g2):
        flags = set()
        nstep = [0]
        a, b, blocked = True, True, False
        while a or b:
            if a:
                try:
                    v = next(g1)
                    if v:
                        flags.add(v)
                except StopIteration:
                    a = False
                    flags.add('SC1')
            nstep[0] += 1
            for _rep in range(2 if nstep[0] % 3 == 0 else 1):
                if b and not (blocked and 'SC1' not in flags):
                    blocked = False
                    try:
                        v = next(g2)
                        if v == 'WAIT_SC1' and 'SC1' not in flags:
                            blocked = True
                    except StopIteration:
                        b = False

    drain(gen_X1(0))
    drain(gen_Sc(0))
    drain(gen_Sd(0))
    for t in range(NT):
        if t + 1 < NT:
            merge(gen_chain(t), seq(gen_A7(t), gen_X1(t + 1), gen_Sc(t + 1), gen_Sd(t + 1)))
        else:
            merge(gen_chain(t), gen_A7(t))
    while ust['done'] < 63:
        u_step()
    v_step(32)

    PARES = ['tk%d' % i for i in range(8)] + ['tki%d' % i for i in range(8)] + ['bv%d' % i for i in range(8)] + ['posu%d' % i for i in range(8)] + ['maskTb', 'resetm', 'bands', 'onesb', 'pwb', 'wppb', 'skT', 'ws0', 'ws1', 'ws2', 'xt0', 'sc', 'eqb', 'hT32', 'hTb', 'qsT',
             'vtm', 'sgT', 'vp', 'gaT', 'gbT', 'E1', 'E2', 'bhT', 'qdT', 'kiT', 'kendT', 'kendm', 'scm', 'S32', 'dec', 'sqb', 'rstd', 'on', 'onT',
             'plT', 'kiT', 'kendT', 'lnrs', 'pld', 'pldb', 'pT', 'kendm', 'tkif', 'tmp128', 'cand', 'cand2', 'posr', 'posc',
             'rf', 'cf', 'i12g', 'i12gb', 'zsum', 'bvs', 'ub2', 'uT2', 'lnst', 'lnst1', 'lnmv', 'lnr'] + ['Sbs%d' % i for i in range(8)]
    pbm = Bump(A, P_END, SB_END)
    G = pbm("G", [128, 256, 128], BF16)
    Pm = [pbm("Pm%d" % i, [128, 32, 128], BF16) for i in range(2)]
    Qm = [pbm("Qm%d" % i, [128, 32, 128], BF16) for i in range(2)]
    h1t = [pbm("h1t%d" % i, [128, 8, 256], BF16) for i in range(2)]
    z0l = pbm("z0l", [128, 2, 1024], F32)
    bc2 = pbm("bc2", [128, 2048], F32)
    ut = [pbm("ut%d" % i, [128, 4, 8, 128], BF16) for i in range(3)]
    vt = [pbm("vt%d" % i, [128, 4, 1024], BF16) for i in range(3)]
    gel = [pbm("gel%d" % i, [128, 256], BF16) for i in range(4)]
    GA = [pbm("GA%d" % i, [128, 256], BF16) for i in range(4)]
    zfin = [pbm("zfin%d" % i, [128, 1024], F32) for i in range(2)]
    st2 = pbm("st2", [128, 2, 6], F32)
    mv2 = pbm("mv2", [128, 2], F32)
    r2 = pbm("r2", [128, 1], F32)

    add('pool', lambda e: e.memset(r2[:, :], 0.0), r=PARES, w=PARES + ['PBGO'])
    dma(bc2[:, :], bc2_d, ['PBGO'], ['bc2'], 'bc2')
    ug = [0]
    for t in range(NT):
        T0 = t * 256
        hs = t % 2
        dma(h1t[hs][:, :, :], h1T_d[:, :, T0:T0 + 256], ['h1T_s', 'PBGO'], ['h1t%d' % hs], 'h1t%d' % hs)
        dma(z0l[:, :, :], z0_d[T0:T0 + 256, :].rearrange("(s p) d -> p s d", p=128), ['z0_s', 'PBGO'], ['z0l'], 'z0l')
        for sbi in range(8):
            s = sbi % 2
            tk0 = T0 + sbi * 32
            iob = iotab[:, :].unsqueeze(1).to_broadcast([128, 32, 128])
            for tl in range(32):
                ts('dve', Pm[s][:, tl, :], iotab[:, :], I1T[:, tk0 + tl:tk0 + tl + 1], GT[:, tk0 + tl:tk0 + tl + 1], ALU.is_equal, ALU.mult,
                   ['idxT', 'iotab', 'PBGO'], ['Pm%d_%d' % (s, tl)])
            tt('dve', Qm[s][:, :, :], I2T[:, tk0:tk0 + 32].unsqueeze(2).to_broadcast([128, 32, 128]), iob, ALU.is_equal,
               ['idxT', 'iotab', 'PBGO'], ['Qm%d' % s])
            for tg in range(8):
                bk = 6 + (tg % 2)
                for j in range(4):
                    tl = tg * 4 + j
                    mm(bank(bk)[:, j * 128:(j + 1) * 128], Qm[s][:, tl, :], Pm[s][:, tl, :], True, True, ['Qm%d' % s, 'Pm%d_%d' % (s, tl)], [PB(bk)])
                cp('act', G[:, sbi * 32 + tg * 4: sbi * 32 + tg * 4 + 4, :], bank(bk).rearrange("p (a b) -> p a b", a=4), [PB(bk), 'PBGO'], ['G'])
        slots = {}

        def issue_A(i1):
            g, b = divmod(i1, 4)
            if b == 0:
                sl = ug[0] % 3
                ug[0] += 1
                slots[g] = sl
                dma(ut[sl][:, :, :, :], uT_d[g * 4:(g + 1) * 4].rearrange("b p k e -> p b k e"), ['uT_s', 'PBGO'], ['ut%d' % sl], 'ut%d' % sl)
                dma(vt[sl][:, :, :], vs_d[g * 4:(g + 1) * 4].rearrange("b e d -> e b d"), ['v_s', 'PBGO'], ['vt%d' % sl], 'vt%d' % sl)
            sl = slots[g]
            ab = 4 + (i1 % 4)
            for k in range(8):
                mm(bank(ab)[:, 0:256], ut[sl][:, b, k, :], h1t[hs][:, k, :], k == 0, k == 7, ['ut%d' % sl, 'h1t%d' % hs], [PB(ab)])
            gs = i1 % 4
            act(gel[gs][:, :], bank(ab)[:, 0:256], AF.Gelu, [PB(ab), 'PBGO'], ['gel%d' % gs])
            tt('dve', GA[gs][:, :], gel[gs][:, :], G[:, :, i1], ALU.mult, ['gel%d' % gs, 'G'], ['GA%d' % gs])

        def issue_out(i1):
            g, b = divmod(i1, 4)
            sl = slots[g]
            gs = i1 % 4
            for sub in range(2):
                for half in range(2):
                    fbk = sub * 2 + half
                    mm(bank(fbk), GA[gs][:, sub * 128:(sub + 1) * 128], vt[sl][:, b, half * 512:(half + 1) * 512], i1 == 0, i1 == 127,
                       ['GA%d' % gs, 'vt%d' % sl], [PB(fbk)])
        issue_A(0)
        issue_A(1)
        for i1 in range(128):
            if i1 + 2 < 128:
                issue_A(i1 + 2)
            issue_out(i1)
        for sub in range(2):
            zf = zfin[sub]
            zr = 'zfin%d' % sub
            tt('dve', zf[:, :], z0l[:, sub, :], ps[:, sub * 1024:(sub + 1) * 1024], ALU.add, ['z0l', PB(sub * 2), PB(sub * 2 + 1), 'PBGO'], [zr])
            add('dve', lambda e, zf=zf: e.bn_stats(out=st2[:, 0, :], in_=zf[:, 0:512]), r=[zr], w=['st2a'])
            add('dve', lambda e, zf=zf: e.bn_stats(out=st2[:, 1, :], in_=zf[:, 512:1024]), r=[zr], w=['st2b'])
            add('dve', lambda e: e.bn_aggr(out=mv2[:, :], in_=st2[:, :, :].rearrange("p a b -> p (a b)")), r=['st2a', 'st2b'], w=['mv2'])
            act(r2[:, :], mv2[:, 1:2], AF.Sqrt, ['mv2', 'PBGO'], ['r2'], bias=LN_EPS)
            add('dve', lambda e: e.reciprocal(out=r2[:, :], in_=r2[:, :]), r=['r2'], w=['r2'])
            ts('dve', zf[:, :], zf[:, :], mv2[:, 0:1], r2[:, 0:1], ALU.subtract, ALU.mult, [zr, 'mv2', 'r2'], [zr])
            tt('pool', zf[:, :], zf[:, :], bc2[:, 0:1024], ALU.mult, [zr, 'bc2'], [zr])
            tt('pool', zf[:, :], zf[:, :], bc2[:, 1024:2048], ALU.add, [zr, 'bc2'], [zr])
            dma(y_d[T0 + sub * 128:T0 + (sub + 1) * 128, :], zf[:, :], [zr], ['y%d' % sub], 'y%d' % sub)
    add('sp', lambda e: e.nop(), r=['y0', 'y1', 'h1T_s', 'z0_s'])
    S.emit(nc, st)
    st.close()
    return nc


def _consts():
    cm = np.zeros((128, 2324), np.float32)
    cm[:, 0:128] = np.eye(128, dtype=np.float32)
    s = np.arange(128)[:, None]
    c = np.arange(128)[None, :]
    cm[:, 128:256] = ((s // 32 == c // 32) & (s <= c)).astype(np.float32)
    cm[:, 256:512] = (np.arange(256) % 32 != 0).astype(np.float32)[None, :]
    wins = (2, 4, 8, 16)
    for g, w in enumerate(wins):
        cur = np.where((s <= c) & (s > c - w), 1.0 / w, 0.0) - (s == c)
        prev = np.where(s > c + 128 - w, 1.0 / w, 0.0)
        cnt = np.minimum(c + 1, w)
        first = np.where((s <= c) & (s > c - w), 1.0 / cnt, 0.0) - (s == c)
        cm[:, 512 + g * 128: 512 + (g + 1) * 128] = cur
        cm[:, 512 + (4 + g) * 128: 512 + (5 + g) * 128] = prev
        cm[:, 512 + (8 + g) * 128: 512 + (9 + g) * 128] = first
    cm[:, 2048:2064] = np.arange(16, dtype=np.float32)[None, :]
    cm[:, 2064:2192] = np.arange(128, dtype=np.float32)[None, :]
    cm[:, 2192:2196] = (np.arange(128)[:, None] // 32 == np.arange(4)[None, :]).astype(np.float32)
    cm[:, 2196:2324] = 1.0 / 128.0
    return cm


def _chunks(w):
    n = w.shape[1] // 512
    return np.ascontiguousarray(w.reshape(8, 128, n, 512).transpose(2, 1, 0, 3))


def _fm(v):
    return np.ascontiguousarray(np.asarray(v, np.float32).reshape(8, 128).T)


def make_in_maps(inputs, ntok=SEQ, ncores=8):
    f = lambda a: np.asarray(a, np.float32)
    wbig = np.concatenate([_chunks(f(inputs["w_in"])[0]), _chunks(f(inputs["w_hg_branch"])[0]), _chunks(f(inputs["w_out"])[0]),
                           _chunks(f(inputs["w_query"])[0]), _chunks(f(inputs["w_ple_gate"])[0])], axis=0)
    wpool = np.ascontiguousarray(f(inputs["pool_w"])[0].transpose(1, 0, 2))
    wpp = np.ascontiguousarray(f(inputs["w_ple_proj"])[0].reshape(2, 128, 1024).transpose(1, 0, 2))
    sk = np.ascontiguousarray(f(inputs["sub_keys"])[0].reshape(16, 128, 128).transpose(1, 0, 2))
    cvec = np.zeros((128, 80), np.float32)
    cvec[:, 0:8] = _fm(inputs["ln0_g"]); cvec[:, 8:16] = _fm(inputs["ln0_b"])
    cvec[:, 16:24] = _fm(f(inputs["ln1_g"])[0]); cvec[:, 24:32] = _fm(f(inputs["ln1_b"])[0])
    cvec[:, 32:40] = _fm(f(inputs["pool_scale"])[0])
    cvec[:, 40:48] = _fm(f(inputs["hg_lb"])[0]); cvec[:, 48:56] = _fm(f(inputs["hg_lb"])[1])
    cvec[:, 56] = f(inputs["hg_norm_g"])[0]
    bc2 = np.ascontiguousarray(np.broadcast_to(np.concatenate([f(inputs["ln2_g"])[0], f(inputs["ln2_b"])[0]])[None, :], (128, 2048)))
    cm = _consts()
    x = f(inputs["x"]); p = f(inputs["p"])[0]
    u = f(inputs["u_tab"])[0]; v = f(inputs["v_tab"])[0]
    maps = []
    for c in range(ncores):
        maps.append(dict(x=np.ascontiguousarray(x[c, :ntok]), p=np.ascontiguousarray(p[c, :ntok]), wbig=wbig, wpool=wpool, wpp=wpp, sk=sk,
                         u_tab=u, v_tab=v, cvec=cvec, cmat=cm, bc2=bc2))
    return maps


def kernel(**inputs):
    nc = build_program(SEQ)
    maps = make_in_maps(inputs, SEQ, 8)
    res = run_bass_kernel_spmd(nc, maps, core_ids=list(range(8)))
    return np.stack([np.asarray(r["y"], np.float32) for r in res.results], axis=0)
```
